# Optimizing a Trainium2 kernel written in Bass

```python
import functools
import jax, jax.numpy as jnp
from jax import lax
import numpy as np

D_MODEL = 4096
BATCH = 32
SEQ = 256
DEPTH = 2
DEC_BATCH = 4
DEC_SEQ = 4096
PAST_LEN = 512

GRID_W = 64
BLOCK = 128
WINDOW = 128
N_HEADS = 16
N_KV_HEADS = 4
HEAD_DIM = 128
Q_PER_KV = N_HEADS // N_KV_HEADS
ATT_W = N_HEADS * HEAD_DIM
KV_W = N_KV_HEADS * HEAD_DIM
ROPE_HALF = HEAD_DIM // 2
ROPE_BASE = 10000.0
A_W = D_MODEL // 4
A_GROUPS = 8
A_GW = A_W // A_GROUPS
CHUNK = 128
B_W = D_MODEL // 4
CONV_W = 3
D_FF = 4 * D_MODEL
ALPHA = (2 * DEPTH) ** 0.25
BETA = (8 * DEPTH) ** -0.25
LN_EPS = 1e-5
SPLIT_SIZES = (ATT_W, KV_W, KV_W, A_W, A_W, B_W, B_W, B_W, D_MODEL, D_MODEL, D_MODEL)
SPLIT_IDX = tuple(int(s) for s in np.cumsum(SPLIT_SIZES)[:-1])
PROJ_W = sum(SPLIT_SIZES)

kernel_name = "prefix_diffusion_hybrid_step"


def layer_norm(x, g, b):
    xf = x.astype(jnp.float32)
    mu = jnp.mean(xf, -1, keepdims=True)
    var = jnp.mean(jnp.square(xf - mu), -1, keepdims=True)
    y = (xf - mu) * lax.rsqrt(var + LN_EPS) * g.astype(jnp.float32) + b.astype(jnp.float32)
    return y.astype(x.dtype)


def rotate_half(z):
    z1, z2 = jnp.split(z, 2, axis=-1)
    return jnp.concatenate([-z2, z1], -1)


def axial_rope(x):
    L = x.shape[1]
    rows = L // GRID_W
    row = jnp.repeat(jnp.arange(rows, dtype=jnp.float32), GRID_W)
    col = jnp.tile(jnp.arange(GRID_W, dtype=jnp.float32), rows)
    inv = 1.0 / (ROPE_BASE ** (jnp.arange(0, ROPE_HALF, 2, dtype=jnp.float32) / ROPE_HALF))

    def rot(z, pos):
        ang = pos[:, None] * inv[None, :]
        ang = jnp.concatenate([ang, ang], -1)[None, :, None, :]
        return z * jnp.cos(ang).astype(z.dtype) + rotate_half(z) * jnp.sin(ang).astype(z.dtype)

    return jnp.concatenate([rot(x[..., :ROPE_HALF], row), rot(x[..., ROPE_HALF:], col)], -1)


def attn_core(q, k, v, sink, mask=None):
    s = jnp.einsum('bqkgd,bskd->bkgqs', q, k).astype(jnp.float32) * (HEAD_DIM ** -0.5)
    if mask is not None:
        s = jnp.where(mask, s, -jnp.inf)
    sk = sink.astype(jnp.float32).reshape(1, N_KV_HEADS, Q_PER_KV, 1, 1)
    m = jnp.maximum(jnp.max(s, -1, keepdims=True), sk)
    p = jnp.exp(s - m)
    denom = jnp.sum(p, -1, keepdims=True) + jnp.exp(sk - m)
    w = (p / denom).astype(v.dtype)
    return jnp.einsum('bkgqs,bskd->bqkgd', w, v)


def to_query_blocks(q):
    B, L = q.shape[:2]
    return q.reshape(B, L // BLOCK, BLOCK, N_KV_HEADS, Q_PER_KV, HEAD_DIM).transpose(1, 0, 2, 3, 4, 5)


def from_query_blocks(o):
    nb, B = o.shape[:2]
    return o.transpose(1, 0, 2, 3, 4, 5).reshape(B, nb * BLOCK, ATT_W)


def context_attention(q, k, v, sink):
    out = lax.map(lambda qb: attn_core(qb, k, v, sink), to_query_blocks(q))
    return from_query_blocks(out)


def latent_attention(q, k, v, ck, cv, sink):
    B, L = q.shape[:2]
    nb = L // BLOCK
    q = axial_rope(q)
    k = axial_rope(k)

    def windows(z):
        zb = jnp.pad(z, ((0, 0), (BLOCK, BLOCK), (0, 0), (0, 0))).reshape(B, nb + 2, BLOCK, N_KV_HEADS, HEAD_DIM)
        zw = jnp.concatenate([zb[:, :-2], zb[:, 1:-1], zb[:, 2:]], axis=2)
        return zw.transpose(1, 0, 2, 3, 4)

    blk = jnp.arange(nb)[:, None, None]
    qpos = blk * BLOCK + jnp.arange(BLOCK)[None, :, None]
    kpos = (blk - 1) * BLOCK + jnp.arange(3 * BLOCK)[None, None, :]
    band = (jnp.abs(kpos - qpos) <= WINDOW) & (kpos >= 0) & (kpos < L)
    ctx_ok = jnp.ones((BLOCK, ck.shape[1]), dtype=bool)

    def body(xs):
        qb, kw, vw, mb = xs
        keys = jnp.concatenate([kw, ck], 1)
        vals = jnp.concatenate([vw, cv], 1)
        return attn_core(qb, keys, vals, sink, jnp.concatenate([mb, ctx_ok], 1))

    out = lax.map(body, (to_query_blocks(q), windows(k), windows(v), band))
    return from_query_blocks(out)


def mixer_a(u, v, g, b, ws, bs):
    B, L, _ = v.shape
    vn = layer_norm(v, g, b).reshape(B, L // CHUNK, CHUNK, A_GROUPS, A_GW)
    s = jnp.einsum('gpq,bnqgc->bnpgc', ws, vn) + bs.T[None, None, :, :, None]
    return u * s.reshape(B, L, A_W)


def mixer_b(bg, cg, hb, w):
    z = cg * hb
    zp = jnp.pad(z, ((0, 0), (1, 1), (0, 0)))
    y = w[0] * zp[:, :-2] + w[1] * zp[:, 1:-1] + w[2] * zp[:, 2:]
    return bg * y


def trunk_layer(x, mod, attn_fn, lw):
    shift1, scale1, gate1, shift2, scale2, gate2 = jnp.split(mod, 6, -1)
    B, L, _ = x.shape
    h = x * (1 + scale1) + shift1
    q, k, v, au, av, bb, bc, bh, g_att, g_a, g_b = jnp.split(h @ lw['w_in'], SPLIT_IDX, -1)
    q = q.reshape(B, L, N_HEADS, HEAD_DIM)
    k = k.reshape(B, L, N_KV_HEADS, HEAD_DIM)
    v = v.reshape(B, L, N_KV_HEADS, HEAD_DIM)
    y_att = attn_fn(q, k, v)
    y_a = mixer_a(au, av, lw['a_ln_g'], lw['a_ln_b'], lw['a_ws'], lw['a_bs'])
    y_b = mixer_b(bb, bc, bh, lw['b_conv'])
    merged = (jax.nn.sigmoid(g_att) * (y_att @ lw['p_att'])
              + jax.nn.sigmoid(g_a) * (y_a @ lw['p_a'])
              + jax.nn.sigmoid(g_b) * (y_b @ lw['p_b']))
    x = layer_norm(ALPHA * x + gate1 * (merged @ lw['w_o']), lw['ln1_g'], lw['ln1_b'])
    hidden = jnp.square(jax.nn.relu((x * (1 + scale2) + shift2) @ lw['w_up']))
    x = layer_norm(ALPHA * x + gate2 * (hidden @ lw['w_down']), lw['ln2_g'], lw['ln2_b'])
    return x, k, v


def setup_inputs(seed: int = 0) -> dict:
    key = jax.random.key(seed)
    ks = jax.random.split(key, 25)

    def nrm(k, shape, scale):
        return jax.random.normal(k, shape, jnp.float32) * scale

    return {
        'x_prompt': nrm(ks[0], (BATCH, SEQ, D_MODEL), 1.0),
        'x_sample': nrm(ks[1], (DEC_BATCH, DEC_SEQ, D_MODEL), 1.0),
        'cache_k': nrm(ks[2], (DEC_BATCH, DEPTH, PAST_LEN, N_KV_HEADS, HEAD_DIM), 1.0),
        'cache_v': nrm(ks[3], (DEC_BATCH, DEPTH, PAST_LEN, N_KV_HEADS, HEAD_DIM), 1.0),
        'c': nrm(ks[4], (DEC_BATCH, D_MODEL), 1.0),
        'c_ctx': nrm(ks[5], (D_MODEL,), 1.0),
        'w_ada': nrm(ks[6], (DEPTH, D_MODEL, 6 * D_MODEL), 0.5 * D_MODEL ** -0.5),
        'b_ada': nrm(ks[7], (DEPTH, 6 * D_MODEL), 0.02),
        'w_in': nrm(ks[8], (DEPTH, D_MODEL, PROJ_W), D_MODEL ** -0.5),
        'w_sink': nrm(ks[9], (DEPTH, N_HEADS), 0.5),
        'a_ln_g': 1.0 + nrm(ks[10], (DEPTH, A_W), 0.02),
        'a_ln_b': nrm(ks[11], (DEPTH, A_W), 0.02),
        'a_ws': nrm(ks[12], (DEPTH, A_GROUPS, CHUNK, CHUNK), CHUNK ** -0.5),
        'a_bs': 1.0 + nrm(ks[13], (DEPTH, A_GROUPS, CHUNK), 0.02),
        'b_conv': nrm(ks[14], (DEPTH, CONV_W, B_W), CONV_W ** -0.5),
        'p_att': nrm(ks[15], (DEPTH, ATT_W, D_MODEL), BETA * ATT_W ** -0.5),
        'p_a': nrm(ks[16], (DEPTH, A_W, D_MODEL), BETA * A_W ** -0.5),
        'p_b': nrm(ks[17], (DEPTH, B_W, D_MODEL), BETA * B_W ** -0.5),
        'w_o': nrm(ks[18], (DEPTH, D_MODEL, D_MODEL), BETA * D_MODEL ** -0.5),
        'ln1_g': 1.0 + nrm(ks[19], (DEPTH, D_MODEL), 0.02),
        'ln1_b': nrm(ks[20], (DEPTH, D_MODEL), 0.02),
        'w_up': nrm(ks[21], (DEPTH, D_MODEL, D_FF), D_MODEL ** -0.5),
        'w_down': nrm(ks[22], (DEPTH, D_FF, D_MODEL), BETA * D_FF ** -0.5),
        'ln2_g': 1.0 + nrm(ks[23], (DEPTH, D_MODEL), 0.02),
        'ln2_b': nrm(ks[24], (DEPTH, D_MODEL), 0.02),
    }


def reference(x_prompt, x_sample, cache_k, cache_v, c, c_ctx, w_ada, b_ada, w_in, w_sink,
              a_ln_g, a_ln_b, a_ws, a_bs, b_conv, p_att, p_a, p_b, w_o,
              ln1_g, ln1_b, w_up, w_down, ln2_g, ln2_b):
    xp = x_prompt
    xs = x_sample
    new_ks = []
    new_vs = []
    for l in range(DEPTH):
        lw = {'w_in': w_in[l], 'a_ln_g': a_ln_g[l], 'a_ln_b': a_ln_b[l], 'a_ws': a_ws[l],
              'a_bs': a_bs[l], 'b_conv': b_conv[l], 'p_att': p_att[l], 'p_a': p_a[l],
              'p_b': p_b[l], 'w_o': w_o[l], 'ln1_g': ln1_g[l], 'ln1_b': ln1_b[l],
              'w_up': w_up[l], 'w_down': w_down[l], 'ln2_g': ln2_g[l], 'ln2_b': ln2_b[l]}
        sink = w_sink[l]
        mod_ctx = (jax.nn.silu(c_ctx) @ w_ada[l] + b_ada[l])[None, None, :]
        xp, k_l, v_l = trunk_layer(xp, mod_ctx, functools.partial(context_attention, sink=sink), lw)
        new_ks.append(k_l)
        new_vs.append(v_l)
        mod_lat = (jax.nn.silu(c) @ w_ada[l] + b_ada[l])[:, None, :]
        lat_attn = functools.partial(latent_attention, ck=cache_k[:, l], cv=cache_v[:, l], sink=sink)
        xs, _, _ = trunk_layer(xs, mod_lat, lat_attn, lw)
    new_k = jnp.stack(new_ks, axis=1)
    new_v = jnp.stack(new_vs, axis=1)
    return (xp, xs, new_k, new_v)
```

```python
import contextlib
import numpy as np
import concourse.bass as bass
import concourse.mybir as mybir
from concourse.bass_utils import run_bass_kernel_spmd

F32 = mybir.dt.float32
BF16 = mybir.dt.bfloat16
ALU = mybir.AluOpType
AF = mybir.ActivationFunctionType

D = 4096
KC = 32
DEPTH = 2
PROJ = 20480
DFF = 16384
ALPHA = (2 * DEPTH) ** 0.25
EPS = 1e-5
NEG = -30000.0
SCALE = 128 ** -0.5
O_Q, O_K, O_V, O_AU, O_AV, O_BB, O_BC, O_BH, O_GATT, O_GA, O_GB = (
    0, 2048, 2560, 3072, 4096, 5120, 6144, 7168, 8192, 12288, 16384)
NRUN = 19
NMAIN = 17
ENGS = ("pe", "act", "dve", "pool", "sp")
N_DMA_SEMS = 12
GRAN = 1024


class Res:
    __slots__ = ("w", "r", "rd", "excl")

    def __init__(self, excl=False):
        self.w = None
        self.r = {}
        self.rd = []
        self.excl = excl


class Op:
    __slots__ = ("eng", "meth", "args", "kw", "deps", "signal", "dma", "dsem", "dval", "sval")

    def __init__(self, eng, meth, args, kw, dma):
        self.eng = eng
        self.meth = meth
        self.args = args
        self.kw = kw
        self.deps = ()
        self.signal = False
        self.dma = dma
        self.dsem = None
        self.dval = 0
        self.sval = 0


class Prog:
    def __init__(self, nc):
        self.nc = nc
        self.ops = {e: [] for e in ENGS}
        self.dma_rr = {"pool": 0, "sp": 0}
        self.dma_last = {}

    def op(self, eng, meth, *args, reads=(), writes=(), dma=False, **kw):
        o = Op(eng, meth, args, kw, dma)
        if any(r.excl for r in reads):
            writes = list(writes) + [r for r in reads if r.excl]
            reads = [r for r in reads if not r.excl]
        deps = {}
        is_pe = (eng == "pe")

        def need(t):
            if t is None:
                return
            if is_pe and t.eng == "pe" and not t.dma:
                return
            deps[id(t)] = t
        for r in reads:
            need(r.w)
        for w in writes:
            need(w.w)
            for t in w.r.values():
                need(t)
            for t in w.rd:
                need(t)
        if dma:
            k = (eng, self.dma_rr[eng] % N_DMA_SEMS)
            self.dma_rr[eng] += 1
            prev = self.dma_last.get(k)
            if prev is not None:
                deps[id(prev)] = prev
            self.dma_last[k] = o
            o.dsem = k
            o.dval = (prev.dval if prev is not None else 0) + 16
        o.deps = tuple(deps.values())
        for t in o.deps:
            if not t.dma:
                t.signal = True
        for r in reads:
            if dma:
                r.rd.append(o)
            else:
                r.r[eng] = o
        for w in writes:
            w.w = o
            w.r = {}
            w.rd = []
        self.ops[eng].append(o)
        return o

    def emit(self):
        nc = self.nc
        with contextlib.ExitStack() as es:
            esem = {e: es.enter_context(nc.semaphore("s_" + e)) for e in ENGS}
            dsem = {}
            for e in ("pool", "sp"):
                for i in range(N_DMA_SEMS):
                    dsem[(e, i)] = es.enter_context(nc.semaphore("d_%s%d" % (e, i)))
            for e in ENGS:
                c = 0
                for o in self.ops[e]:
                    if o.signal and not o.dma:
                        c += 1
                        o.sval = c
            block = es.enter_context(nc.Block())

            def run(e, eng):
                waited = {}
                for o in self.ops[e]:
                    for t in o.deps:
                        if t.dma:
                            key, s, v = t.dsem, dsem[t.dsem], t.dval
                        else:
                            key, s, v = t.eng, esem[t.eng], t.sval
                        if waited.get(key, 0) >= v:
                            continue
                        waited[key] = v
                        eng.wait_ge(s, v)
                    ins = getattr(eng, o.meth)(*o.args, **o.kw)
                    if o.dma:
                        ins.then_inc(dsem[o.dsem], 16)
                    elif o.signal:
                        ins.then_inc(esem[e], 1)
                if e == "sp":
                    for k, o in self.dma_last.items():
                        eng.wait_ge(dsem[k], o.dval)

            block.tensor(lambda eng: run("pe", eng))
            block.scalar(lambda eng: run("act", eng))
            block.vector(lambda eng: run("dve", eng))
            block.gpsimd(lambda eng: run("pool", eng))
            block.sync(lambda eng: run("sp", eng))


class Region:
    def __init__(self, nc, es, name, nbytes):
        self.t32 = es.enter_context(nc.sbuf_tensor(name, [128, nbytes // 4], F32))
        self.t16 = self.t32.bitcast(BF16)
        self.nbytes = nbytes
        self.g = [Res() for _ in range((nbytes + GRAN - 1) // GRAN)]

    def view(self, dt, off, shape):
        es_ = 4 if dt == F32 else 2
        n = int(np.prod(shape))
        assert off % es_ == 0 and off + n * es_ <= self.nbytes, (off, n, es_, self.nbytes)
        t = self.t32 if dt == F32 else self.t16
        e0 = off // es_
        ap = t[:, e0:e0 + n]
        if len(shape) == 2:
            ap = ap.rearrange("p (a b) -> p a b", a=shape[0])
        elif len(shape) == 3:
            ap = ap.rearrange("p (a b c) -> p a b c", a=shape[0], b=shape[1])
        res = self.g[off // GRAN:(off + n * es_ - 1) // GRAN + 1]
        return ap, res


class Buf:
    def __init__(self, region, dt, off, shape):
        self.region = region
        self.dt = dt
        self.off = off
        self.shape = tuple(shape)
        self.ap, self.res = region.view(dt, off, shape)
        self.es = 4 if dt == F32 else 2

    def sub(self, i):
        n = int(np.prod(self.shape[1:])) * self.es
        o = self.off + i * n
        return self.region.g[o // GRAN:(o + n - 1) // GRAN + 1]

    def subr(self, i0, i1):
        n = int(np.prod(self.shape[1:])) * self.es
        o0 = self.off + i0 * n
        o1 = self.off + i1 * n
        return self.region.g[o0 // GRAN:(o1 - 1) // GRAN + 1]


def build(cfg=None):
    cfg = cfg or {}
    layers = cfg.get("layers", list(range(DEPTH)))
    do_prompt = cfg.get("prompt_tiles", [0, 1])
    do_sample = cfg.get("sample", True)

    nc = bass.Bass("TRN2", target_bir_lowering=False)

    def din(name, shape, dt=F32):
        return nc.dram_tensor(name, list(shape), dt, kind="ExternalInput").ap()

    def dout(name, shape, dt=F32):
        return nc.dram_tensor(name, list(shape), dt, kind="ExternalOutput").ap()

    def dscr(name, shape, dt=F32):
        return nc.dram_tensor(name, list(shape), dt).ap()

    xp_d = din("xp", [1024, D])
    xs_d = din("xs", [NRUN * 128, D])
    ckv_d = din("ckv", [2, DEPTH, 512, 512])
    cvec_d = din("cvec", [128, KC, 2])
    bada_d = din("bada", [DEPTH, 128, 192])
    lnp_d = din("lnp", [128, DEPTH, 4, KC])
    alng_d = din("alng", [128, DEPTH, 8])
    alnb_d = din("alnb", [DEPTH, 1, 1024])
    awsT_d = din("awsT", [DEPTH, 128, 8, 128])
    abs_d = din("abs", [DEPTH, 1, 1024])
    bconv_d = din("bconv", [128, DEPTH, 3, 8])
    sinkb_d = din("sinkb", [128, DEPTH, 16])
    rope_d = din("rope", [2, 128, NRUN * 128])
    rmat_d = din("rmat", [128, 128])
    ident_d = din("ident", [128, 128])
    maskb_d = din("maskb", [128, 2, 512])
    hfl_d = din("hfl", [128, 4])
    w_ada = din("w_ada", [DEPTH, D, 6 * D])
    w_in = din("w_in", [DEPTH, D, PROJ])
    p_att = din("p_att", [DEPTH, 2048, D])
    p_a = din("p_a", [DEPTH, 1024, D])
    p_b = din("p_b", [DEPTH, 1024, D])
    w_o = din("w_o", [DEPTH, D, D])
    w_up = din("w_up", [DEPTH, D, DFF])
    w_down = din("w_down", [DEPTH, DFF, D])

    yp_d = dout("yp", [1024, D])
    ys_d = dout("ys", [NMAIN * 128, D])
    nk_d = dout("nk", [4, DEPTH, 256, 512])
    nv_d = dout("nv", [4, DEPTH, 256, 512])

    mk_mid = dout if cfg.get("debug") else dscr
    xmp_d = mk_mid("xmid_p", [1024, D])
    xms_d = mk_mid("xmid_s", [NRUN * 128, D])
    resid_d = dscr("resid", [2, 128, KC, 512])
    kscr_d = dscr("kscr", [128, 4, NRUN * 128], BF16)
    vscr_d = dscr("vscr", [NRUN * 128, 512], BF16)

    P = Prog(nc)
    es = contextlib.ExitStack()
    with es:
        RP = Region(nc, es, "RP", 65536)
        RQ = Region(nc, es, "RQ", 32768)
        RS = Region(nc, es, "RS", 49152)
        RW = Region(nc, es, "RW", 32768)
        RC = Region(nc, es, "RC", 30720)
        pst = [es.enter_context(nc.psum_tensor("ps%d" % i, [128, 512], F32)) for i in range(8)]
        psR = [Res(excl=True) for _ in range(8)]
        ps_free = list(range(8))

        def ps_alloc():
            assert ps_free, "out of PSUM banks"
            return ps_free.pop(0)

        def ps_release(i):
            ps_free.append(i)

        coff = [0]

        def cbuf(dt, shape):
            n = int(np.prod(shape)) * (4 if dt == F32 else 2)
            n_al = (n + 31) // 32 * 32
            b = Buf(RC, dt, coff[0], shape)
            coff[0] += n_al
            return b
        modT = cbuf(F32, [DEPTH * 2, 192])
        s1p = cbuf(F32, [DEPTH * 2, KC])
        s2a = cbuf(F32, [DEPTH * 2, KC])
        lnp = cbuf(F32, [DEPTH * 4, KC])
        lnA = cbuf(F32, [DEPTH * 2, KC])
        alng = cbuf(F32, [DEPTH, 8])
        bconv = cbuf(F32, [DEPTH * 3, 8])
        esink = cbuf(F32, [DEPTH, 16])
        bias2 = cbuf(F32, [8, 128])
        wsT = cbuf(BF16, [8, 128])
        ident = cbuf(F32, [128])
        identb = cbuf(BF16, [128])
        onesb = cbuf(BF16, [128])
        ones32 = cbuf(F32, [128])
        rmat = cbuf(F32, [128])
        maskb = cbuf(BF16, [2, 512])
        hfl = cbuf(F32, [8])
        ckT = cbuf(BF16, [4, 512])
        cvb = cbuf(BF16, [4, 512])
        hB = cbuf(BF16, [KC, 16])
        bcB = cbuf(F32, [8, 16])
        zB = cbuf(F32, [8, 16])
        svec = cbuf(BF16, [KC, 2])
        cvec = cbuf(F32, [KC, 2])
        bada = cbuf(F32, [192])
        st6 = cbuf(F32, [2, 6])
        mv = cbuf(F32, [4])
        assert coff[0] <= RC.nbytes, coff[0]

        def col(buf, i, j):
            return buf.ap[:, i, j:j + 1]

        wslots = [Buf(RW, BF16, i * 8192, [4096]) for i in range(4)]
        wrr = [0]

        def wload(src_ap, nk, ncols):
            s = wslots[wrr[0] % 4]
            wrr[0] += 1
            view = s.ap[:, 0:nk * ncols].rearrange("p (k n) -> p k n", k=nk)
            src = src_ap.rearrange("(k p) n -> p k n", p=128)
            P.op("pool", "dma_start", out=view, in_=src, writes=s.res, dma=True)
            return view, s.res

        deferred = []

        def flush_deferred():
            while deferred:
                deferred.pop(0)()

        def stream_fm(wsrc, nk, ncols, rhs_fn, N, evac, defer_evac_pe=None):
            nout = ncols // 128
            pk = min(nk, 4096 // ncols)
            banks = [ps_alloc() for _ in range(nout)]
            for k0 in range(0, nk, pk):
                wv, wres = wload(wsrc(k0, pk), pk, ncols)
                for j in range(nout):
                    b = banks[j]
                    for kl in range(pk):
                        kc = k0 + kl
                        rhs, rres = rhs_fn(kc)
                        P.op("pe", "matmul",
                            pst[b][:, 0:N], wv[:, kl, j * 128:(j + 1) * 128], rhs,
                            start=(kc == 0), stop=(kc == nk - 1),
                            reads=list(wres) + list(rres), writes=[psR[b]])
            flush_deferred()
            for j in range(nout):
                evac(j, banks[j])
                ps_release(banks[j])

        def stream_tm(wsrc, nk, ncols, lhs_fn, nblk, evac):
            pk = min(nk, 4096 // ncols)
            banks = [ps_alloc() for _ in range(nblk)]
            for k0 in range(0, nk, pk):
                wv, wres = wload(wsrc(k0, pk), pk, ncols)
                for t in range(nblk):
                    b = banks[t]
                    for kl in range(pk):
                        kc = k0 + kl
                        lhs, lres = lhs_fn(kc, t)
                        P.op("pe", "matmul",
                            pst[b][:, 0:ncols], lhs, wv[:, kl, :],
                            start=(kc == 0), stop=(kc == nk - 1),
                            reads=list(wres) + list(lres), writes=[psR[b]])
            flush_deferred()
            for t in range(nblk):
                evac(t, banks[t])
                ps_release(banks[t])

        def wsrc2(w3, l, c0, ncols, r0=0):
            return lambda k0, n: w3[l, r0 + k0 * 128: r0 + (k0 + n) * 128, c0:c0 + ncols]

        def sp_load(buf_ap, res, src):
            P.op("sp", "dma_start", out=buf_ap, in_=src, writes=res, dma=True)

        sp_load(ident.ap, ident.res, ident_d)
        sp_load(rmat.ap, rmat.res, rmat_d)
        sp_load(hfl.ap[:, 0:4], hfl.res, hfl_d)
        sp_load(cvec.ap, cvec.res, cvec_d)
        sp_load(lnp.ap.rearrange("p (l f) k -> p l f k", l=DEPTH), lnp.res, lnp_d)
        sp_load(alng.ap, alng.res, alng_d)
        sp_load(bconv.ap.rearrange("p (l f) k -> p l f k", l=DEPTH), bconv.res, bconv_d)
        sp_load(esink.ap, esink.res, sinkb_d)
        P.op("pool", "dma_start", out=maskb.ap, in_=maskb_d, writes=maskb.res, dma=True)
        P.op("dve", "memset", onesb.ap, 1.0, writes=onesb.res)
        P.op("dve", "memset", ones32.ap, 1.0, writes=ones32.res)
        P.op("dve", "memset", hfl.ap[:, 4:5], NEG, reads=hfl.res, writes=hfl.res)
        P.op("dve", "memset", hfl.ap[:, 5:6], EPS, reads=hfl.res, writes=hfl.res)
        P.op("dve", "memset", hfl.ap[:, 6:7], 0.0, reads=hfl.res, writes=hfl.res)
        P.op("dve", "tensor_copy", identb.ap, ident.ap, reads=ident.res, writes=identb.res)
        P.op("act", "activation", esink.ap, esink.ap, AF.Exp, reads=esink.res, writes=esink.res)
        P.op("act", "activation", svec.ap, cvec.ap, AF.Silu, reads=cvec.res, writes=svec.res)
        P.op("dve", "tensor_scalar", lnA.ap.rearrange("p (l f) k -> p l f k", l=DEPTH),
                                              lnp.ap.rearrange("p (l f) k -> p l f k", l=DEPTH)[:, :, 0:2, :],
                                              ALPHA, None, op0=ALU.mult, reads=lnp.res, writes=lnA.res)

        def mod_stage(l):
            sp_load(bada.ap, bada.res, bada_d[l])
            for cg in range(48):
                def evac(j, b, cg=cg):
                    c = cg * 4 + j
                    P.op("dve", "tensor_scalar",
                        modT.ap[:, 2 * l:2 * l + 2, c], pst[b][:, 0:2], bada.ap[:, c:c + 1], None, op0=ALU.add,
                        reads=[psR[b]] + bada.res, writes=modT.res)
                stream_fm(wsrc2(w_ada, l, cg * 512, 512), KC, 512,
                          lambda kc: (svec.ap[:, kc, :], svec.res), 2, evac)
            for g in range(2):
                i = 2 * l + g
                P.op("dve", "tensor_scalar", s1p.ap[:, i, :], modT.ap[:, i, 32:64], 1.0, None,
                                                         op0=ALU.add, reads=modT.res, writes=s1p.res)
                P.op("dve", "tensor_scalar", s2a.ap[:, i, :], modT.ap[:, i, 128:160], 1.0, 1.0 / ALPHA,
                                                         op0=ALU.add, op1=ALU.mult, reads=modT.res, writes=s2a.res)

        dbg_d = dout("dbg", [128, 4096]) if cfg.get("debug") else None

        def dbg_dump(ap, res, n):
            P.op("sp", "dma_start", out=dbg_d[:, 0:n], in_=ap, reads=res, dma=True)

        for l in layers:
            mod_stage(l)
        if cfg.get("stop") == "mod":
            dbg_dump(modT.ap.rearrange("p a b -> p (a b)"), modT.res, 768)
            P.emit()
            return nc

        def layer_setup(l):
            P.op("pool", "dma_start", out=wsT.ap, in_=awsT_d[l], writes=wsT.res, dma=True)
            ws32 = Buf(RS, F32, 0, [8, 128])
            L2 = Buf(RS, F32, 4096, [8, 128])
            R2 = Buf(RS, F32, 8192, [8, 128])
            sp_load(ws32.ap, ws32.res, awsT_d[l])
            P.op("dve", "memset", L2.ap[0:2], 1.0, writes=L2.res)
            P.op("sp", "dma_start", out=L2.ap[0:1], in_=alnb_d[l].rearrange("o (g c) -> o g c", g=8),
                 reads=L2.res, writes=L2.res, dma=True)
            P.op("sp", "dma_start", out=R2.ap[1:2], in_=abs_d[l].rearrange("o (g c) -> o g c", g=8),
                 writes=R2.res, dma=True)
            for g0 in range(0, 8, 4):
                b = ps_alloc()
                for g in range(g0, g0 + 4):
                    P.op("pe", "matmul", pst[b][0:1, (g - g0) * 128:(g - g0 + 1) * 128],
                                                         ones32.ap[:, 0:1], ws32.ap[:, g, :], start=True, stop=True,
                         reads=ones32.res + ws32.res, writes=[psR[b]])
                P.op("dve", "tensor_copy",
                    R2.ap[0:1, g0:g0 + 4, :], pst[b][0:1, :].rearrange("p (a c) -> p a c", a=4),
                    reads=[psR[b]] + R2.res, writes=R2.res)
                ps_release(b)
            for g0 in range(0, 8, 4):
                b = ps_alloc()
                for g in range(g0, g0 + 4):
                    P.op("pe", "matmul", pst[b][:, (g - g0) * 128:(g - g0 + 1) * 128],
                                                         L2.ap[0:2, g, :], R2.ap[0:2, g, :], start=True, stop=True,
                         reads=L2.res + R2.res, writes=[psR[b]])
                P.op("dve", "tensor_copy",
                    bias2.ap[:, g0:g0 + 4, :], pst[b][:, :].rearrange("p (a c) -> p a c", a=4),
                    reads=[psR[b]], writes=bias2.res)
                ps_release(b)

        hT = Buf(RP, BF16, 0, [KC, 512])
        yT = Buf(RP, BF16, 32768, [KC, 512])
        y1T = Buf(RP, F32, 0, [KC, 512])
        mT = Buf(RQ, BF16, 0, [KC, 512])

        def stage_x(src_rows, nb, mi, spill_par, dep_res=()):
            T = nb * 128
            xrow = [Buf(RS, F32, b * 16384, [D]) if b < 3 else Buf(RQ, F32, 0, [D]) for b in range(nb)]
            rst = [Buf(RQ, F32, 16384 + i * 2048, [512]) for i in range(4)]
            for b in range(nb):
                P.op("sp", "dma_start", out=xrow[b].ap, in_=src_rows(b),
                     reads=list(dep_res[b:b + 1]), writes=xrow[b].res, dma=True)
            for j in range(KC):
                bk = ps_alloc()
                for b in range(nb):
                    P.op("pe", "transpose",
                        pst[bk][:, b * 128:(b + 1) * 128], xrow[b].ap[:, j * 128:(j + 1) * 128], ident.ap,
                        reads=xrow[b].res[j // 2:j // 2 + 1] + ident.res, writes=[psR[bk]])
                P.op("act", "activation",
                    hT.ap[:, j, 0:T], pst[bk][:, 0:T], AF.Identity, bias=col(modT, mi, j), scale=col(s1p, mi, j),
                    reads=[psR[bk]] + modT.res + s1p.res, writes=hT.sub(j))
                if spill_par is not None:
                    r = rst[j % 4]
                    P.op("dve", "tensor_scalar", r.ap[:, 0:T], pst[bk][:, 0:T], ALPHA, None,
                                                                    op0=ALU.mult, reads=[psR[bk]], writes=r.res)
                    P.op("sp", "dma_start", out=resid_d[spill_par, :, j, 0:T], in_=r.ap[:, 0:T],
                         reads=r.res, writes=[residR[spill_par][j]], dma=True)
                ps_release(bk)

        residR = [[Res() for _ in range(KC)] for _ in range(2)]
        xmidR = {"p": [Res() for _ in range(8)], "s": [Res() for _ in range(NRUN)]}
        kscrR = [Res() for _ in range(NRUN)]
        vscrR = [Res() for _ in range(NRUN)]

        qTg = [Buf(RS, BF16, i * 4096, [4, 512]) for i in range(2)]
        kTt = Buf(RS, BF16, 8192, [4, 768])
        vt = Buf(RS, BF16, 14336, [6, 512])
        ptb = [Buf(RS, BF16, 20480 + i * 1024, [512]) for i in range(4)]
        f32t = [Buf(RS, F32, 24576 + i * 2048, [512]) for i in range(8)]
        ropet = Buf(RS, F32, 40960, [2, 512])

        def attention(g, kind, nb, l, b0, qbuf):
            for qb in range(nb):
                keys = []
                if kind == "p":
                    s = qb // 2
                    for kb in (2 * s, 2 * s + 1):
                        keys.append((kTt.ap[:, g, kb * 128:(kb + 1) * 128], kTt.res,
                                     vt.ap[:, kb, g * 128:(g + 1) * 128], vt.sub(kb), None, hfl.ap[:, 6:7]))
                else:
                    for w in range(3):
                        wi = qb + w
                        ab = b0 - 1 + wi
                        bias = hfl.ap[:, 6:7]
                        if ab == 0:
                            bias = hfl.ap[:, 0:1] if l == 0 else hfl.ap[:, 4:5]
                        elif ab == NRUN - 1:
                            bias = hfl.ap[:, 1:2] if l == 0 else hfl.ap[:, 4:5]
                        keys.append((kTt.ap[:, g, wi * 128:(wi + 1) * 128], kTt.res,
                                     vt.ap[:, wi, g * 128:(g + 1) * 128], vt.sub(wi),
                                     (0 if w == 0 else (1 if w == 2 else None)), bias))
                    for cb in range(4):
                        keys.append((ckT.ap[:, g, cb * 128:(cb + 1) * 128], ckT.res,
                                     cvb.ap[:, cb, g * 128:(g + 1) * 128], cvb.res, None, hfl.ap[:, 6:7]))
                nkb = len(keys)
                bN = ps_alloc()
                bD = ps_alloc()
                q_rhs = qbuf.ap[:, :, qb * 128:(qb + 1) * 128]
                sb = {}

                def do_s(i):
                    kT_ap, kres, _, _, mi_, _ = keys[i]
                    b = ps_alloc()
                    sb[i] = b
                    P.op("pe", "matmul", pst[b][:, :], kT_ap, q_rhs,
                                                  start=True, stop=(mi_ is None),
                         reads=list(kres) + qbuf.res, writes=[psR[b]])
                    if mi_ is not None:
                        P.op("pe", "matmul", pst[b][:, :], identb.ap, maskb.ap[:, mi_, :],
                                                      start=False, stop=True,
                             reads=identb.res + maskb.res, writes=[psR[b]])

                def do_pv(i):
                    _, _, v_ap, vres, _, bias = keys[i]
                    b = sb.pop(i)
                    pt = ptb[(qb * nkb + i) % 4]
                    P.op("act", "activation", pt.ap, pst[b][:, :], AF.Exp, bias=bias, scale=SCALE,
                         reads=[psR[b]] + hfl.res, writes=pt.res)
                    ps_release(b)
                    P.op("pe", "matmul", pst[bN][:, :], v_ap, pt.ap, start=(i == 0), stop=(i == nkb - 1),
                         reads=list(vres) + pt.res, writes=[psR[bN]])
                    P.op("pe", "matmul", pst[bD][:, :], onesb.ap, pt.ap, start=(i == 0), stop=(i == nkb - 1),
                         reads=onesb.res + pt.res, writes=[psR[bD]])
                do_s(0)
                if nkb > 1:
                    do_s(1)
                for i in range(nkb):
                    do_pv(i)
                    if i + 2 < nkb:
                        do_s(i + 2)
                den = f32t[6 + (qb % 2)]
                for h in range(4):
                    P.op("dve", "tensor_scalar",
                        den.ap[:, h * 128:(h + 1) * 128], pst[bD][:, h * 128:(h + 1) * 128],
                        col(esink, l, 4 * g + h), None, op0=ALU.add,
                        reads=[psR[bD]] + esink.res, writes=den.res)
                P.op("dve", "reciprocal", den.ap, den.ap, reads=den.res, writes=den.res)
                P.op("dve", "tensor_tensor",
                    yT.ap[:, 4 * g:4 * g + 4, qb * 128:(qb + 1) * 128],
                    pst[bN][:, :].rearrange("p (a c) -> p a c", a=4),
                    den.ap.rearrange("p (a c) -> p a c", a=4), op=ALU.mult,
                    reads=[psR[bN]] + den.res, writes=yT.subr(4 * g, 4 * g + 4))
                ps_release(bN)
                ps_release(bD)

        def rope_evac(dst_ap, dst_res, bank, T, tmp_i):
            qf = f32t[tmp_i % 2]
            t1 = f32t[2 + tmp_i % 2]
            t2 = f32t[4 + tmp_i % 2]
            P.op("act", "copy", qf.ap[:, 0:T], pst[bank][:, 0:T], reads=[psR[bank]], writes=qf.res)
            rb = ps_alloc()

            def pe_part():
                P.op("pe", "matmul", pst[rb][:, 0:T], rmat.ap, qf.ap[:, 0:T], start=True, stop=True,
                     reads=rmat.res + qf.res, writes=[psR[rb]])
                P.op("dve", "tensor_tensor", t2.ap[:, 0:T], pst[rb][:, 0:T], ropet.ap[:, 1, 0:T], op=ALU.mult,
                     reads=[psR[rb]] + ropet.res, writes=t2.res)
                ps_release(rb)
                P.op("dve", "tensor_tensor", t1.ap[:, 0:T], qf.ap[:, 0:T], ropet.ap[:, 0, 0:T], op=ALU.mult,
                     reads=qf.res + ropet.res, writes=t1.res)
                P.op("dve", "tensor_tensor", dst_ap, t1.ap[:, 0:T], t2.ap[:, 0:T], op=ALU.add,
                     reads=t1.res + t2.res, writes=dst_res)
            pe_part()

        MT = [(1, 4), (5, 4), (9, 3), (12, 3), (15, 3)]
        BND = []
        for (b0_, nb_) in MT:
            BND += [b0_ * 128 - 1, (b0_ + nb_) * 128]

        def prepass(l, src_d):
            mi = 2 * l + 1
            if l == 0:
                ptiles = [(0, 4), (4, 4), (8, 4), (12, 4), (16, 3)]
            else:
                ptiles = [(1, 4), (5, 4), (9, 4), (13, 4), (17, 1)]
            P.op("dve", "memset", hB.ap, 0.0, writes=hB.res)
            for (b0, nb) in ptiles:
                T = nb * 128
                stage_x(lambda b, b0=b0: src_d[(b0 + b) * 128:(b0 + b + 1) * 128, :], nb, mi, None,
                        [xmidR["s"][b0 + b] for b in range(nb)] if l > 0 else ())
                sp_load(ropet.ap[:, :, 0:T], ropet.res,
                        rope_d[:, :, b0 * 128:b0 * 128 + T].rearrange("a p t -> p a t"))
                for bi, tok in enumerate(BND):
                    if b0 * 128 <= tok < b0 * 128 + T:
                        P.op("dve", "tensor_copy",
                            hB.ap[:, :, bi:bi + 1], hT.ap[:, :, tok - b0 * 128:tok - b0 * 128 + 1],
                            reads=hT.res, writes=hB.res)
                kst = Buf(RS, BF16, 45056, [4, 512])

                def k_evac(j, bank, T=T, b0=b0, kst=kst):
                    rope_evac(kst.ap[:, j, 0:T], kst.res, bank, T, j)
                stream_fm(wsrc2(w_in, l, O_K, 512), KC, 512, lambda kc: (hT.ap[:, kc, 0:T], hT.sub(kc)), T, k_evac)
                P.op("sp", "dma_start",
                    out=kscr_d[:, :, b0 * 128:b0 * 128 + T], in_=kst.ap[:, :, 0:T],
                    reads=kst.res, writes=kscrR[b0:b0 + nb], dma=True)

                def v_evac(t, bank, b0=b0):
                    vs = ptb[t % 4]
                    P.op("act", "copy", vs.ap, pst[bank][:, :], reads=[psR[bank]], writes=vs.res)
                    P.op("sp", "dma_start", out=vscr_d[(b0 + t) * 128:(b0 + t + 1) * 128, :], in_=vs.ap,
                         reads=vs.res, writes=[vscrR[b0 + t]], dma=True)
                stream_tm(wsrc2(w_in, l, O_V, 512), KC, 512,
                          lambda kc, t: (hT.ap[:, kc, t * 128:(t + 1) * 128], hT.sub(kc)), nb, v_evac)
            for which, off in ((0, O_BC), (1, O_BH)):
                for grp in range(2):
                    def evac(j, bank, which=which, grp=grp):
                        c = grp * 4 + j
                        if which == 0:
                            P.op("act", "copy", bcB.ap[:, c, :], pst[bank][:, 0:16],
                                 reads=[psR[bank]], writes=bcB.res)
                        else:
                            P.op("dve", "tensor_tensor", zB.ap[:, c, :], pst[bank][:, 0:16], bcB.ap[:, c, :],
                                                                  op=ALU.mult,
                                 reads=[psR[bank]] + bcB.res, writes=zB.res)
                    stream_fm(wsrc2(w_in, l, off + grp * 512, 512), KC, 512,
                              lambda kc: (hB.ap[:, kc, :], hB.res), 16, evac)
            zv0 = hfl.ap[:, 2:3] if l == 0 else hfl.ap[:, 6:7]
            zv1 = hfl.ap[:, 3:4] if l == 0 else hfl.ap[:, 6:7]
            P.op("dve", "tensor_scalar", zB.ap[:, :, 0:1], zB.ap[:, :, 0:1], zv0, None, op0=ALU.mult,
                 reads=zB.res + hfl.res, writes=zB.res)
            P.op("dve", "tensor_scalar", zB.ap[:, :, 9:10], zB.ap[:, :, 9:10], zv1, None, op0=ALU.mult,
                 reads=zB.res + hfl.res, writes=zB.res)

        def sample_ctx_setup(l):
            P.op("pool", "dma_start", out=cvb.ap, in_=ckv_d[1, l].rearrange("(b p) n -> p b n", p=128),
                 writes=cvb.res, dma=True)
            ckf = Buf(RS, F32, 0, [4, 512])
            sp_load(ckf.ap, ckf.res, ckv_d[0, l].rearrange("(b p) n -> p b n", p=128))
            for cb in range(4):
                bk = ps_alloc()
                for h in range(4):
                    P.op("pe", "transpose",
                        pst[bk][:, h * 128:(h + 1) * 128], ckf.ap[:, cb, h * 128:(h + 1) * 128], ident.ap,
                        reads=ckf.res + ident.res, writes=[psR[bk]])
                P.op("act", "copy",
                    ckT.ap[:, :, cb * 128:(cb + 1) * 128], pst[bk][:, :].rearrange("p (a c) -> p a c", a=4),
                    reads=[psR[bk]], writes=ckT.res)
                ps_release(bk)

        tile_ctr = [0]

        def main_tile(l, kind, ti):
            par = tile_ctr[0] % 2
            tile_ctr[0] += 1
            if kind == "p":
                nb, b0 = 4, ti * 4
                mi = 2 * l
                src_d = xp_d if l == 0 else xmp_d
                dst_d = xmp_d if l < DEPTH - 1 else yp_d
                src_rows = lambda b: src_d[(b0 + b) * 128:(b0 + b + 1) * 128, :]
                dst_rows = lambda b: dst_d[(b0 + b) * 128:(b0 + b + 1) * 128, :]
                srcR = lambda b: xmidR["p"][b0 + b]
            else:
                b0, nb = MT[ti]
                mi = 2 * l + 1
                src_d = xs_d if l == 0 else xms_d
                src_rows = lambda b: src_d[(b0 + b) * 128:(b0 + b + 1) * 128, :]
                if l < DEPTH - 1:
                    dst_rows = lambda b: xms_d[(b0 + b) * 128:(b0 + b + 1) * 128, :]
                else:
                    dst_rows = lambda b: ys_d[(b0 - 1 + b) * 128:(b0 + b) * 128, :]
                srcR = lambda b: xmidR["s"][b0 + b]
            T = nb * 128
            gate1 = lambda c: col(modT, mi, 64 + c)
            shift2 = lambda c: col(modT, mi, 96 + c)
            gate2 = lambda c: col(modT, mi, 160 + c)
            hrhs = lambda kc: (hT.ap[:, kc, 0:T], hT.sub(kc))

            stage_x_deps = [srcR(b) for b in range(nb)] if l > 0 else []
            stage_x(src_rows, nb, mi, par, stage_x_deps)

            if kind == "p":
                def k_evac(t, bank):
                    kf = f32t[t % 2]
                    P.op("act", "copy", kf.ap, pst[bank][:, :], reads=[psR[bank]], writes=kf.res)
                    s, r = (b0 + t) // 2, ((b0 + t) % 2) * 128
                    P.op("sp", "dma_start", out=nk_d[s, l, r:r + 128, :], in_=kf.ap, reads=kf.res, dma=True)
                    bk = ps_alloc()
                    for h in range(4):
                        P.op("pe", "transpose", pst[bk][:, h * 128:(h + 1) * 128],
                                                             kf.ap[:, h * 128:(h + 1) * 128], ident.ap,
                             reads=kf.res + ident.res, writes=[psR[bk]])
                    P.op("dve", "tensor_copy", kTt.ap[:, :, t * 128:(t + 1) * 128],
                                                        pst[bk][:, :].rearrange("p (a c) -> p a c", a=4),
                         reads=[psR[bk]], writes=kTt.res)
                    ps_release(bk)
                stream_tm(wsrc2(w_in, l, O_K, 512), KC, 512,
                          lambda kc, t: (hT.ap[:, kc, t * 128:(t + 1) * 128], hT.sub(kc)), nb, k_evac)

                def v_evac(t, bank):
                    vf = f32t[2 + t % 2]
                    P.op("act", "copy", vf.ap, pst[bank][:, :], reads=[psR[bank]], writes=vf.res)
                    s, r = (b0 + t) // 2, ((b0 + t) % 2) * 128
                    P.op("sp", "dma_start", out=nv_d[s, l, r:r + 128, :], in_=vf.ap, reads=vf.res, dma=True)
                    P.op("dve", "tensor_copy", vt.ap[:, t, :], pst[bank][:, :],
                         reads=[psR[bank]], writes=vt.sub(t))
                stream_tm(wsrc2(w_in, l, O_V, 512), KC, 512,
                          lambda kc, t: (hT.ap[:, kc, t * 128:(t + 1) * 128], hT.sub(kc)), nb, v_evac)
            else:
                nw = nb + 2
                P.op("sp", "dma_start", out=kTt.ap[:, :, 0:nw * 128],
                                                 in_=kscr_d[:, :, (b0 - 1) * 128:(b0 - 1 + nw) * 128],
                     reads=kscrR[b0 - 1:b0 - 1 + nw], writes=kTt.res, dma=True)
                P.op("sp", "dma_start",
                    out=vt.ap[:, 0:nw, :],
                    in_=vscr_d[(b0 - 1) * 128:(b0 - 1 + nw) * 128, :].rearrange("(b p) n -> p b n", p=128),
                    reads=vscrR[b0 - 1:b0 - 1 + nw], writes=vt.res, dma=True)
                sp_load(ropet.ap[:, :, 0:T], ropet.res,
                        rope_d[:, :, b0 * 128:b0 * 128 + T].rearrange("a p t -> p a t"))

            for g in range(4):
                qbuf = qTg[g % 2]

                def q_evac(j, bank, qbuf=qbuf):
                    if kind == "p":
                        P.op("act", "copy", qbuf.ap[:, j, 0:T], pst[bank][:, 0:T],
                             reads=[psR[bank]], writes=qbuf.res)
                    else:
                        rope_evac(qbuf.ap[:, j, 0:T], qbuf.res, bank, T, j)
                stream_fm(wsrc2(w_in, l, O_Q + g * 512, 512), KC, 512, hrhs, T, q_evac)
                deferred.append(lambda g=g, qbuf=qbuf: attention(g, kind, nb, l, b0, qbuf))

            avs = Buf(RS, F32, 0, [4, 512])
            zA = Buf(RS, BF16, 8192, [4, 1024])
            tA = Buf(RS, F32, 24576, [8, 512])

            def av0_evac(t, bank):
                P.op("act", "copy", avs.ap[:, t, :], pst[bank][:, :], reads=[psR[bank]], writes=avs.sub(t))
            stream_tm(wsrc2(w_in, l, O_AV, 512), KC, 512,
                      lambda kc, t: (hT.ap[:, kc, t * 128:(t + 1) * 128], hT.sub(kc)), nb, av0_evac)

            def av1_evac(t, bank):
                P.op("dve", "bn_stats", st6.ap[:, 0, :], avs.ap[:, t, :], reads=avs.sub(t), writes=st6.res)
                P.op("dve", "bn_stats", st6.ap[:, 1, :], pst[bank][:, :], reads=[psR[bank]] + st6.res,
                     writes=st6.res)
                P.op("dve", "bn_aggr", mv.ap[:, 0:2], st6.ap, reads=st6.res, writes=mv.res)
                P.op("act", "activation", mv.ap[:, 2:3], mv.ap[:, 1:2], AF.Sqrt, bias=hfl.ap[:, 5:6], scale=1.0,
                     reads=mv.res + hfl.res, writes=mv.res)
                P.op("dve", "reciprocal", mv.ap[:, 3:4], mv.ap[:, 2:3], reads=mv.res, writes=mv.res)
                P.op("dve", "tensor_scalar", zA.ap[:, t, 0:512], avs.ap[:, t, :], mv.ap[:, 0:1], mv.ap[:, 3:4],
                                                      op0=ALU.subtract, op1=ALU.mult,
                     reads=avs.sub(t) + mv.res, writes=zA.sub(t))
                P.op("dve", "tensor_scalar", zA.ap[:, t, 512:1024], pst[bank][:, :], mv.ap[:, 0:1], mv.ap[:, 3:4],
                                                      op0=ALU.subtract, op1=ALU.mult,
                     reads=[psR[bank]] + mv.res, writes=zA.sub(t))
            stream_tm(wsrc2(w_in, l, O_AV + 512, 512), KC, 512,
                      lambda kc, t: (hT.ap[:, kc, t * 128:(t + 1) * 128], hT.sub(kc)), nb, av1_evac)

            def s_part():
                for g in range(8):
                    bk = ps_alloc()
                    for t in range(nb):
                        P.op("pe", "matmul",
                            pst[bk][:, t * 128:(t + 1) * 128], zA.ap[:, t, g * 128:(g + 1) * 128], wsT.ap[:, g, :],
                            start=True, stop=True, reads=zA.sub(t) + wsT.res, writes=[psR[bk]])
                    P.op("dve", "scalar_tensor_tensor",
                        tA.ap[:, g, 0:T].rearrange("p (a c) -> p a c", a=nb),
                        pst[bk][:, 0:T].rearrange("p (a c) -> p a c", a=nb), col(alng, l, g),
                        bias2.ap[:, g:g + 1, :].broadcast_to([128, nb, 128]), op0=ALU.mult, op1=ALU.add,
                        reads=[psR[bk]] + alng.res + bias2.res, writes=tA.sub(g))
                    ps_release(bk)
            deferred.append(s_part)

            for grp in range(2):
                def au_evac(j, bank, grp=grp):
                    c = grp * 4 + j
                    P.op("dve", "tensor_tensor", yT.ap[:, 16 + c, 0:T], pst[bank][:, 0:T], tA.ap[:, c, 0:T],
                                                          op=ALU.mult,
                         reads=[psR[bank]] + tA.sub(c), writes=yT.sub(16 + c))
                stream_fm(wsrc2(w_in, l, O_AU + grp * 512, 512), KC, 512, hrhs, T, au_evac)

            bhT = Buf(RS, F32, 0, [8, 512])
            zT = Buf(RS, F32, 16384, [8, 516])
            for grp in range(2):
                def bh_evac(j, bank, grp=grp):
                    c = grp * 4 + j
                    P.op("act", "copy", bhT.ap[:, c, 0:T], pst[bank][:, 0:T], reads=[psR[bank]],
                         writes=bhT.sub(c))
                stream_fm(wsrc2(w_in, l, O_BH + grp * 512, 512), KC, 512, hrhs, T, bh_evac)
            if kind == "s":
                P.op("dve", "tensor_copy", zT.ap[:, :, 0:1], zB.ap[:, :, 2 * ti:2 * ti + 1],
                     reads=zB.res, writes=zT.res)
                P.op("dve", "tensor_copy", zT.ap[:, :, T + 1:T + 2], zB.ap[:, :, 2 * ti + 1:2 * ti + 2],
                     reads=zB.res, writes=zT.res)
                segs = [(0, T, True)]
            else:
                segs = [(0, 256, False), (256, 512, False)]
            for grp in range(2):
                def bc_evac(j, bank, grp=grp):
                    c = grp * 4 + j
                    P.op("dve", "tensor_tensor", zT.ap[:, c, 1:T + 1], pst[bank][:, 0:T], bhT.ap[:, c, 0:T],
                                                          op=ALU.mult,
                         reads=[psR[bank]] + bhT.sub(c), writes=zT.sub(c))
                    w0, w1, w2 = col(bconv, 3 * l, c), col(bconv, 3 * l + 1, c), col(bconv, 3 * l + 2, c)
                    for (a, b_, halo) in segs:
                        P.op("dve", "tensor_scalar",
                            bhT.ap[:, c, a:b_], zT.ap[:, c, 1 + a:1 + b_], w1, None, op0=ALU.mult,
                            reads=zT.sub(c) + bconv.res, writes=bhT.sub(c))
                        lo = 0 if halo else 1
                        P.op("dve", "scalar_tensor_tensor",
                            bhT.ap[:, c, a + lo:b_], zT.ap[:, c, a + lo:b_], w0, bhT.ap[:, c, a + lo:b_],
                            op0=ALU.mult, op1=ALU.add,
                            reads=zT.sub(c) + bhT.sub(c) + bconv.res, writes=bhT.sub(c))
                        P.op("dve", "scalar_tensor_tensor",
                            bhT.ap[:, c, a:b_ - lo], zT.ap[:, c, a + 2:b_ + 2 - lo], w2, bhT.ap[:, c, a:b_ - lo],
                            op0=ALU.mult, op1=ALU.add,
                            reads=zT.sub(c) + bhT.sub(c) + bconv.res, writes=bhT.sub(c))
                stream_fm(wsrc2(w_in, l, O_BC + grp * 512, 512), KC, 512, hrhs, T, bc_evac)
            for grp in range(2):
                def bb_evac(j, bank, grp=grp):
                    c = grp * 4 + j
                    P.op("dve", "tensor_tensor", yT.ap[:, 24 + c, 0:T], pst[bank][:, 0:T], bhT.ap[:, c, 0:T],
                                                          op=ALU.mult,
                         reads=[psR[bank]] + bhT.sub(c), writes=yT.sub(24 + c))
                stream_fm(wsrc2(w_in, l, O_BB + grp * 512, 512), KC, 512, hrhs, T, bb_evac)

            sg = Buf(RS, F32, 0, [12, 512])
            mt = Buf(RS, F32, 24576, [4, 512])
            tmpm = Buf(RS, F32, 32768, [4, 512])
            for G in range(8):
                for gi, off in enumerate((O_GATT, O_GA, O_GB)):
                    def g_evac(j, bank, gi=gi):
                        P.op("act", "activation", sg.ap[:, gi * 4 + j, 0:T], pst[bank][:, 0:T], AF.Sigmoid,
                             reads=[psR[bank]], writes=sg.sub(gi * 4 + j))
                    stream_fm(wsrc2(w_in, l, off + G * 512, 512), KC, 512, hrhs, T, g_evac)

                def pa_evac(j, bank):
                    P.op("dve", "tensor_tensor", mt.ap[:, j, 0:T], pst[bank][:, 0:T], sg.ap[:, j, 0:T], op=ALU.mult,
                         reads=[psR[bank]] + sg.sub(j), writes=mt.sub(j))
                stream_fm(wsrc2(p_att, l, G * 512, 512), 16, 512, lambda kc: (yT.ap[:, kc, 0:T], yT.sub(kc)), T, pa_evac)

                def pb_evac(j, bank):
                    P.op("dve", "tensor_tensor", tmpm.ap[:, j, 0:T], pst[bank][:, 0:T], sg.ap[:, 4 + j, 0:T],
                                                          op=ALU.mult,
                         reads=[psR[bank]] + sg.sub(4 + j), writes=tmpm.sub(j))
                    P.op("dve", "tensor_tensor", mt.ap[:, j, 0:T], mt.ap[:, j, 0:T], tmpm.ap[:, j, 0:T], op=ALU.add,
                         reads=mt.sub(j) + tmpm.sub(j), writes=mt.sub(j))
                stream_fm(wsrc2(p_a, l, G * 512, 512), 8, 512, lambda kc: (yT.ap[:, 16 + kc, 0:T], yT.sub(16 + kc)), T,
                          pb_evac)

                def pc_evac(j, bank, G=G):
                    P.op("dve", "tensor_tensor", tmpm.ap[:, j, 0:T], pst[bank][:, 0:T], sg.ap[:, 8 + j, 0:T],
                                                          op=ALU.mult,
                         reads=[psR[bank]] + sg.sub(8 + j), writes=tmpm.sub(j))
                    P.op("dve", "tensor_tensor", mT.ap[:, 4 * G + j, 0:T], mt.ap[:, j, 0:T], tmpm.ap[:, j, 0:T],
                                                          op=ALU.add,
                         reads=mt.sub(j) + tmpm.sub(j), writes=mT.sub(4 * G + j))
                stream_fm(wsrc2(p_b, l, G * 512, 512), 8, 512, lambda kc: (yT.ap[:, 24 + kc, 0:T], yT.sub(24 + kc)), T,
                          pc_evac)

            rin = [Buf(RS, F32, i * 2048, [512]) for i in range(4)]
            sqt = [Buf(RS, F32, 8192 + i * 2048, [512]) for i in range(4)]
            stt = [Buf(RS, F32, 16384 + i * 2048, [512]) for i in range(4)]
            nrm = [Buf(RS, F32, 24576 + i * 2048, [512]) for i in range(4)]

            def ln_stats_mm(c, src_ap, src_res, bS, bQ, sq):
                P.op("act", "activation", sq.ap[:, 0:T], src_ap, AF.Square, reads=src_res, writes=sq.res)

                def pe_part():
                    P.op("pe", "matmul", pst[bS][:, 0:T], ones32.ap, src_ap, start=(c == 0), stop=(c == KC - 1),
                         reads=ones32.res + list(src_res), writes=[psR[bS]])
                    P.op("pe", "matmul", pst[bQ][:, 0:T], ones32.ap, sq.ap[:, 0:T], start=(c == 0),
                                                  stop=(c == KC - 1),
                         reads=ones32.res + sq.res, writes=[psR[bQ]])
                deferred.append(pe_part)

            def ln_finish(bS, bQ):
                mean, rstd, msq = stt[0], stt[1], stt[2]
                P.op("dve", "tensor_scalar", mean.ap[:, 0:T], pst[bS][:, 0:T], 1.0 / D, None, op0=ALU.mult,
                     reads=[psR[bS]], writes=mean.res)
                P.op("dve", "tensor_tensor", msq.ap[:, 0:T], mean.ap[:, 0:T], mean.ap[:, 0:T], op=ALU.mult,
                     reads=mean.res, writes=msq.res)
                P.op("dve", "scalar_tensor_tensor", rstd.ap[:, 0:T], pst[bQ][:, 0:T], 1.0 / D, msq.ap[:, 0:T],
                                                             op0=ALU.mult, op1=ALU.subtract,
                     reads=[psR[bQ]] + msq.res, writes=rstd.res)
                P.op("act", "activation", rstd.ap[:, 0:T], rstd.ap[:, 0:T], AF.Sqrt, bias=hfl.ap[:, 5:6], scale=1.0,
                     reads=rstd.res + hfl.res, writes=rstd.res)
                P.op("dve", "reciprocal", rstd.ap[:, 0:T], rstd.ap[:, 0:T], reads=rstd.res, writes=rstd.res)
                ps_release(bS)
                ps_release(bQ)
                return mean, rstd

            bS, bQ = ps_alloc(), ps_alloc()
            for G in range(8):
                for j in range(4):
                    c = 4 * G + j
                    P.op("sp", "dma_start", out=rin[j].ap[:, 0:T], in_=resid_d[par, :, c, 0:T],
                         reads=[residR[par][c]], writes=rin[j].res, dma=True)

                def o_evac(j, bank, G=G):
                    c = 4 * G + j
                    P.op("dve", "scalar_tensor_tensor", y1T.ap[:, c, 0:T], pst[bank][:, 0:T], gate1(c),
                                                                 rin[j].ap[:, 0:T], op0=ALU.mult, op1=ALU.add,
                         reads=[psR[bank]] + rin[j].res + modT.res, writes=y1T.sub(c))
                    ln_stats_mm(c, y1T.ap[:, c, 0:T], y1T.sub(c), bS, bQ, sqt[c % 4])
                stream_fm(wsrc2(w_o, l, G * 512, 512), KC, 512, lambda kc: (mT.ap[:, kc, 0:T], mT.sub(kc)), T, o_evac)
            flush_deferred()
            mean, rstd = ln_finish(bS, bQ)
            for c in range(KC):
                t1, t2 = nrm[c % 2], nrm[2 + c % 2]
                P.op("dve", "tensor_tensor", t1.ap[:, 0:T], y1T.ap[:, c, 0:T], mean.ap[:, 0:T],
                                                                  op=ALU.subtract,
                     reads=y1T.sub(c) + mean.res, writes=t1.res)
                P.op("dve", "scalar_tensor_tensor",
                    t2.ap[:, 0:T], t1.ap[:, 0:T], col(lnA, 2 * l, c), rstd.ap[:, 0:T], op0=ALU.mult, op1=ALU.mult,
                    reads=t1.res + rstd.res + lnA.res, writes=t2.res)
                P.op("act", "activation", y1T.ap[:, c, 0:T], t2.ap[:, 0:T], AF.Identity,
                                                             bias=col(lnA, 2 * l + 1, c), scale=1.0,
                     reads=t2.res + lnA.res, writes=y1T.sub(c))
                P.op("act", "activation", mT.ap[:, c, 0:T], y1T.ap[:, c, 0:T], AF.Identity,
                                                      bias=shift2(c), scale=col(s2a, mi, c),
                     reads=y1T.sub(c) + modT.res + s2a.res, writes=mT.sub(c))

            hid = [Buf(RS, BF16, i * 16384, [16, 512]) for i in range(2)]
            rtmp = [Buf(RS, F32, 32768 + i * 2048, [512]) for i in range(2)]
            h2rhs = lambda kc: (mT.ap[:, kc, 0:T], mT.sub(kc))
            for fb in range(8):
                hb = hid[fb % 2]
                for ug in range(4):
                    def up_evac(j, bank, ug=ug, hb=hb):
                        c = 4 * ug + j
                        rt = rtmp[c % 2]
                        P.op("act", "activation", rt.ap[:, 0:T], pst[bank][:, 0:T], AF.Relu,
                             reads=[psR[bank]], writes=rt.res)
                        P.op("dve", "tensor_tensor", hb.ap[:, c, 0:T], rt.ap[:, 0:T], rt.ap[:, 0:T], op=ALU.mult,
                             reads=rt.res, writes=hb.sub(c))
                    stream_fm(wsrc2(w_up, l, fb * 2048 + ug * 512, 512), KC, 512, h2rhs, T, up_evac)
                if fb == 7:
                    sq2 = [Buf(RS, F32, 36864 + i * 2048, [512]) for i in range(4)]
                    bS2, bQ2 = ps_alloc(), ps_alloc()
                for dg in range(8):
                    def dn_evac(j, bank, dg=dg, fb=fb):
                        c = 4 * dg + j
                        P.op("dve", "scalar_tensor_tensor", y1T.ap[:, c, 0:T], pst[bank][:, 0:T], gate2(c),
                                                                     y1T.ap[:, c, 0:T], op0=ALU.mult, op1=ALU.add,
                             reads=[psR[bank]] + y1T.sub(c) + modT.res, writes=y1T.sub(c))
                        if fb == 7:
                            ln_stats_mm(c, y1T.ap[:, c, 0:T], y1T.sub(c), bS2, bQ2, sq2[c % 4])
                    stream_fm(wsrc2(w_down, l, dg * 512, 512, r0=fb * 2048), 16, 512,
                              lambda kc, hb=hb: (hb.ap[:, kc, 0:T], hb.sub(kc)), T, dn_evac)

            flush_deferred()
            stt2 = [Buf(RS, F32, 45056, [512]), Buf(RS, F32, 47104, [512]), Buf(RS, F32, 32768, [512])]
            stt[0], stt[1], stt[2] = stt2
            mean, rstd = ln_finish(bS2, bQ2)
            nrm2 = [Buf(RS, F32, 34816 + i * 2048, [512]) for i in range(3)]
            for c in range(KC):
                t1 = nrm2[c % 3]
                P.op("dve", "tensor_tensor", t1.ap[:, 0:T], y1T.ap[:, c, 0:T], mean.ap[:, 0:T],
                                                                  op=ALU.subtract,
                     reads=y1T.sub(c) + mean.res, writes=t1.res)
                P.op("dve", "scalar_tensor_tensor",
                    t1.ap[:, 0:T], t1.ap[:, 0:T], col(lnp, 4 * l + 2, c), rstd.ap[:, 0:T], op0=ALU.mult, op1=ALU.mult,
                    reads=t1.res + rstd.res + lnp.res, writes=t1.res)
                P.op("act", "activation", y1T.ap[:, c, 0:T], t1.ap[:, 0:T], AF.Identity,
                                                             bias=col(lnp, 4 * l + 3, c), scale=1.0,
                     reads=t1.res + lnp.res, writes=y1T.sub(c))
            ost = [Buf(RS, F32, i * 16384, [D]) for i in range(2)]
            for b in range(nb):
                o = ost[b % 2]
                for G in range(8):
                    bk = ps_alloc()
                    for j in range(4):
                        c = 4 * G + j
                        P.op("pe", "transpose",
                            pst[bk][:, j * 128:(j + 1) * 128], y1T.ap[:, c, b * 128:(b + 1) * 128], ident.ap,
                            reads=y1T.sub(c) + ident.res, writes=[psR[bk]])
                    if G % 2 == 0:
                        P.op("act", "copy", o.ap[:, G * 512:(G + 1) * 512], pst[bk][:, :],
                             reads=[psR[bk]], writes=o.res[2 * G:2 * G + 2])
                    else:
                        P.op("dve", "tensor_copy", o.ap[:, G * 512:(G + 1) * 512], pst[bk][:, :],
                             reads=[psR[bk]], writes=o.res[2 * G:2 * G + 2])
                    ps_release(bk)
                P.op("sp", "dma_start", out=dst_rows(b), in_=o.ap, reads=o.res,
                     writes=[srcR(b)], dma=True)

        if cfg.get("group_major", True):
            if do_prompt:
                for l in layers:
                    layer_setup(l)
                    for ti in do_prompt:
                        main_tile(l, "p", ti)
            if do_sample:
                for l in layers:
                    layer_setup(l)
                    prepass(l, xs_d if l == 0 else xms_d)
                    sample_ctx_setup(l)
                    for ti in cfg.get("sample_tiles", range(5)):
                        main_tile(l, "s", ti)
        else:
            for l in layers:
                layer_setup(l)
                if do_sample:
                    prepass(l, xs_d if l == 0 else xms_d)
                for ti in do_prompt:
                    main_tile(l, "p", ti)
                if do_sample:
                    sample_ctx_setup(l)
                    for ti in cfg.get("sample_tiles", range(5)):
                        main_tile(l, "s", ti)
        flush_deferred()
        P.emit()
    return nc


def _rope_tables(base_blk):
    pos = (base_blk * 128 + np.arange(NRUN * 128)).astype(np.int64)
    pos = np.clip(pos, 0, 4095)
    row = (pos // 64).astype(np.float32)
    colp = (pos % 64).astype(np.float32)
    inv = (1.0 / (np.float32(10000.0) ** (np.arange(0, 64, 2, dtype=np.float32) / np.float32(64)))).astype(np.float32)
    ang_r = (row[:, None] * inv[None, :]).astype(np.float32)
    ang_c = (colp[:, None] * inv[None, :]).astype(np.float32)
    ang = np.concatenate([ang_r, ang_r, ang_c, ang_c], axis=1)
    tab = np.stack([np.cos(ang).astype(np.float32).T, np.sin(ang).astype(np.float32).T], 0)
    return np.ascontiguousarray(tab)


def _rmat():
    m = np.zeros((128, 128), np.float32)
    for d in range(128):
        if d % 64 < 32:
            m[d + 32, d] = -1.0
        else:
            m[d - 32, d] = 1.0
    return m


def _masks():
    k = np.arange(128)[:, None]
    q = np.arange(128)[None, :]
    mp = np.where(k >= q, 0.0, NEG).astype(np.float32)
    mn = np.where(k <= q, 0.0, NEG).astype(np.float32)
    out = np.zeros((128, 2, 512), np.float32)
    out[:, 0, :] = np.tile(mp, (1, 4))
    out[:, 1, :] = np.tile(mn, (1, 4))
    return out


def prep_inputs(inp, n_cores=8):
    f = lambda a: np.ascontiguousarray(np.asarray(a, dtype=np.float32))
    x_prompt, x_sample = f(inp["x_prompt"]), f(inp["x_sample"])
    cache_k, cache_v = f(inp["cache_k"]), f(inp["cache_v"])
    c, c_ctx = f(inp["c"]), f(inp["c_ctx"])
    shared = {k: f(inp[k]) for k in ("w_ada", "w_in", "p_att", "p_a", "p_b", "w_o", "w_up", "w_down")}
    fm = lambda v, n: np.ascontiguousarray(v.reshape(n, 128).T)
    b_ada = f(inp["b_ada"])
    shared["bada"] = np.stack([fm(b_ada[l], 192) for l in range(DEPTH)], 0)
    lnp = np.zeros((128, DEPTH, 4, KC), np.float32)
    for l in range(DEPTH):
        for i, k in enumerate(("ln1_g", "ln1_b", "ln2_g", "ln2_b")):
            lnp[:, l, i, :] = fm(f(inp[k])[l], KC)
    shared["lnp"] = lnp
    shared["alng"] = np.ascontiguousarray(np.stack([fm(f(inp["a_ln_g"])[l], 8) for l in range(DEPTH)], 1))
    shared["alnb"] = f(inp["a_ln_b"]).reshape(DEPTH, 1, 1024)
    a_ws = f(inp["a_ws"])
    shared["awsT"] = np.ascontiguousarray(a_ws.transpose(0, 3, 1, 2))
    shared["abs"] = f(inp["a_bs"]).reshape(DEPTH, 1, 1024)
    b_conv = f(inp["b_conv"])
    bc = np.zeros((128, DEPTH, 3, 8), np.float32)
    for l in range(DEPTH):
        for i in range(3):
            bc[:, l, i, :] = fm(b_conv[l, i], 8)
    shared["bconv"] = bc
    shared["sinkb"] = np.ascontiguousarray(np.broadcast_to(f(inp["w_sink"])[None], (128, DEPTH, 16)))
    shared["rmat"] = _rmat()
    shared["ident"] = np.eye(128, dtype=np.float32)
    shared["maskb"] = _masks()
    ropes = {-1: _rope_tables(-1), 14: _rope_tables(14)}
    maps = []
    for core in range(n_cores):
        b, half = core // 2, core % 2
        base = -1 if half == 0 else 14
        xs = np.zeros((NRUN * 128, D), np.float32)
        for i in range(NRUN):
            sb = base + i
            if 0 <= sb < 32:
                xs[i * 128:(i + 1) * 128] = x_sample[b, sb * 128:(sb + 1) * 128]
        hfl = np.zeros((128, 4), np.float32)
        if half == 0:
            hfl[:, 0], hfl[:, 1], hfl[:, 2], hfl[:, 3] = NEG, 0.0, 0.0, 1.0
        else:
            hfl[:, 0], hfl[:, 1], hfl[:, 2], hfl[:, 3] = 0.0, NEG, 1.0, 0.0
        cvec = np.zeros((128, KC, 2), np.float32)
        cvec[:, :, 0] = fm(c_ctx, KC)
        cvec[:, :, 1] = fm(c[b], KC)
        m = dict(shared)
        m["xp"] = np.ascontiguousarray(x_prompt[4 * core:4 * core + 4].reshape(1024, D))
        m["xs"] = xs
        m["ckv"] = np.ascontiguousarray(np.stack([cache_k[b].reshape(DEPTH, 512, 512),
                                                  cache_v[b].reshape(DEPTH, 512, 512)], 0))
        m["cvec"] = cvec
        m["rope"] = ropes[base]
        m["hfl"] = hfl
        maps.append(m)
    return maps


def assemble(results, n_cores=8):
    y_prompt = np.zeros((32, 256, D), np.float32)
    y_sample = np.zeros((4, 4096, D), np.float32)
    new_k = np.zeros((32, DEPTH, 256, 4, 128), np.float32)
    new_v = np.zeros((32, DEPTH, 256, 4, 128), np.float32)
    for core in range(n_cores):
        r = results[core]
        b, half = core // 2, core % 2
        y_prompt[4 * core:4 * core + 4] = r["yp"].reshape(4, 256, D)
        if half == 0:
            y_sample[b, 0:2048] = r["ys"][0:2048]
        else:
            y_sample[b, 2048:4096] = r["ys"][128:2176]
        new_k[4 * core:4 * core + 4] = r["nk"].reshape(4, DEPTH, 256, 4, 128)
        new_v[4 * core:4 * core + 4] = r["nv"].reshape(4, DEPTH, 256, 4, 128)
    return y_prompt, y_sample, new_k, new_v


def kernel(**inputs):
    maps = prep_inputs(inputs)
    nc = build()
    res = run_bass_kernel_spmd(nc, maps, core_ids=list(range(8)))
    return assemble(res.results)
```

```python
import contextlib
import numpy as np
import concourse.bass as bass
import concourse.mybir as mybir
from concourse.bass_utils import run_bass_kernel_spmd

F32 = mybir.dt.float32
BF16 = mybir.dt.bfloat16
ALU = mybir.AluOpType
AF = mybir.ActivationFunctionType

D = 4096
KC = 32
DEPTH = 2
PROJ = 20480
DFF = 16384
ALPHA = (2 * DEPTH) ** 0.25
EPS = 1e-5
NEG = -30000.0
SCALE = 128 ** -0.5
O_Q, O_K, O_V, O_AU, O_AV, O_BB, O_BC, O_BH, O_GATT, O_GA, O_GB = (
    0, 2048, 2560, 3072, 4096, 5120, 6144, 7168, 8192, 12288, 16384)
NRUN = 19
NMAIN = 17
ENGS = ("pe", "act", "dve", "pool", "sp")
N_DMA_SEMS = 12
GRAN = 1024


class Res:
    __slots__ = ("w", "r", "rd", "excl")

    def __init__(self, excl=False):
        self.w = None
        self.r = {}
        self.rd = []
        self.excl = excl


class Op:
    __slots__ = ("eng", "meth", "args", "kw", "deps", "signal", "dma", "dsem", "dval", "sval")

    def __init__(self, eng, meth, args, kw, dma):
        self.eng = eng
        self.meth = meth
        self.args = args
        self.kw = kw
        self.deps = ()
        self.signal = False
        self.dma = dma
        self.dsem = None
        self.dval = 0
        self.sval = 0


class Prog:
    def __init__(self, nc):
        self.nc = nc
        self.ops = {e: [] for e in ENGS}
        self.dma_rr = {"pool": 0, "sp": 0}
        self.dma_last = {}

    def op(self, eng, meth, *args, reads=(), writes=(), dma=False, **kw):
        o = Op(eng, meth, args, kw, dma)
        if any(r.excl for r in reads):
            writes = list(writes) + [r for r in reads if r.excl]
            reads = [r for r in reads if not r.excl]
        deps = {}
        is_pe = (eng == "pe")

        def need(t):
            if t is None:
                return
            if is_pe and t.eng == "pe" and not t.dma:
                return
            deps[id(t)] = t
        for r in reads:
            need(r.w)
        for w in writes:
            need(w.w)
            for t in w.r.values():
                need(t)
            for t in w.rd:
                need(t)
        if dma:
            k = (eng, self.dma_rr[eng] % N_DMA_SEMS)
            self.dma_rr[eng] += 1
            prev = self.dma_last.get(k)
            if prev is not None:
                deps[id(prev)] = prev
            self.dma_last[k] = o
            o.dsem = k
            o.dval = (prev.dval if prev is not None else 0) + 16
        o.deps = tuple(deps.values())
        for t in o.deps:
            if not t.dma:
                t.signal = True
        for r in reads:
            if dma:
                r.rd.append(o)
            else:
                r.r[eng] = o
        for w in writes:
            w.w = o
            w.r = {}
            w.rd = []
        self.ops[eng].append(o)
        return o

    def emit(self):
        nc = self.nc
        with contextlib.ExitStack() as es:
            esem = {e: es.enter_context(nc.semaphore("s_" + e)) for e in ENGS}
            dsem = {}
            for e in ("pool", "sp"):
                for i in range(N_DMA_SEMS):
                    dsem[(e, i)] = es.enter_context(nc.semaphore("d_%s%d" % (e, i)))
            for e in ENGS:
                c = 0
                for o in self.ops[e]:
                    if o.signal and not o.dma:
                        c += 1
                        o.sval = c
            block = es.enter_context(nc.Block())

            def run(e, eng):
                waited = {}
                for o in self.ops[e]:
                    for t in o.deps:
                        if t.dma:
                            key, s, v = t.dsem, dsem[t.dsem], t.dval
                        else:
                            key, s, v = t.eng, esem[t.eng], t.sval
                        if waited.get(key, 0) >= v:
                            continue
                        waited[key] = v
                        eng.wait_ge(s, v)
                    ins = getattr(eng, o.meth)(*o.args, **o.kw)
                    if o.dma:
                        ins.then_inc(dsem[o.dsem], 16)
                    elif o.signal:
                        ins.then_inc(esem[e], 1)
                if e == "sp":
                    for k, o in self.dma_last.items():
                        eng.wait_ge(dsem[k], o.dval)

            block.tensor(lambda eng: run("pe", eng))
            block.scalar(lambda eng: run("act", eng))
            block.vector(lambda eng: run("dve", eng))
            block.gpsimd(lambda eng: run("pool", eng))
            block.sync(lambda eng: run("sp", eng))


class Region:
    def __init__(self, nc, es, name, nbytes):
        self.t32 = es.enter_context(nc.sbuf_tensor(name, [128, nbytes // 4], F32))
        self.t16 = self.t32.bitcast(BF16)
        self.nbytes = nbytes
        self.g = [Res() for _ in range((nbytes + GRAN - 1) // GRAN)]

    def view(self, dt, off, shape):
        es_ = 4 if dt == F32 else 2
        n = int(np.prod(shape))
        assert off % es_ == 0 and off + n * es_ <= self.nbytes, (off, n, es_, self.nbytes)
        t = self.t32 if dt == F32 else self.t16
        e0 = off // es_
        ap = t[:, e0:e0 + n]
        if len(shape) == 2:
            ap = ap.rearrange("p (a b) -> p a b", a=shape[0])
        elif len(shape) == 3:
            ap = ap.rearrange("p (a b c) -> p a b c", a=shape[0], b=shape[1])
        res = self.g[off // GRAN:(off + n * es_ - 1) // GRAN + 1]
        return ap, res


class Buf:
    def __init__(self, region, dt, off, shape):
        self.region = region
        self.dt = dt
        self.off = off
        self.shape = tuple(shape)
        self.ap, self.res = region.view(dt, off, shape)
        self.es = 4 if dt == F32 else 2

    def sub(self, i):
        n = int(np.prod(self.shape[1:])) * self.es
        o = self.off + i * n
        return self.region.g[o // GRAN:(o + n - 1) // GRAN + 1]

    def subr(self, i0, i1):
        n = int(np.prod(self.shape[1:])) * self.es
        o0 = self.off + i0 * n
        o1 = self.off + i1 * n
        return self.region.g[o0 // GRAN:(o1 - 1) // GRAN + 1]


def build(cfg=None):
    cfg = cfg or {}
    layers = cfg.get("layers", list(range(DEPTH)))
    do_prompt = cfg.get("prompt_tiles", [0, 1])
    do_sample = cfg.get("sample", True)

    nc = bass.Bass("TRN2", target_bir_lowering=False)

    def din(name, shape, dt=F32):
        return nc.dram_tensor(name, list(shape), dt, kind="ExternalInput").ap()

    def dout(name, shape, dt=F32):
        return nc.dram_tensor(name, list(shape), dt, kind="ExternalOutput").ap()

    def dscr(name, shape, dt=F32):
        return nc.dram_tensor(name, list(shape), dt).ap()

    xp_d = din("xp", [1024, D])
    xs_d = din("xs", [NRUN * 128, D])
    ckv_d = din("ckv", [2, DEPTH, 512, 512])
    cvec_d = din("cvec", [128, KC, 2])
    bada_d = din("bada", [DEPTH, 128, 192])
    lnp_d = din("lnp", [128, DEPTH, 4, KC])
    alng_d = din("alng", [128, DEPTH, 8])
    alnb_d = din("alnb", [DEPTH, 1, 1024])
    awsT_d = din("awsT", [DEPTH, 128, 8, 128])
    abs_d = din("abs", [DEPTH, 1, 1024])
    bconv_d = din("bconv", [128, DEPTH, 3, 8])
    sinkb_d = din("sinkb", [128, DEPTH, 16])
    rope_d = din("rope", [2, 128, NRUN * 128])
    rmat_d = din("rmat", [128, 128])
    ident_d = din("ident", [128, 128])
    maskb_d = din("maskb", [128, 2, 512])
    hfl_d = din("hfl", [128, 4])
    w_ada = din("w_ada", [DEPTH, D, 6 * D])
    w_in = din("w_in", [DEPTH, D, PROJ])
    p_att = din("p_att", [DEPTH, 2048, D])
    p_a = din("p_a", [DEPTH, 1024, D])
    p_b = din("p_b", [DEPTH, 1024, D])
    w_o = din("w_o", [DEPTH, D, D])
    w_up = din("w_up", [DEPTH, D, DFF])
    w_down = din("w_down", [DEPTH, DFF, D])

    yp_d = dout("yp", [1024, D])
    ys_d = dout("ys", [NMAIN * 128, D])
    nk_d = dout("nk", [4, DEPTH, 256, 512])
    nv_d = dout("nv", [4, DEPTH, 256, 512])

    mk_mid = dout if cfg.get("debug") else dscr
    xmp_d = mk_mid("xmid_p", [1024, D])
    xms_d = mk_mid("xmid_s", [NRUN * 128, D])
    resid_d = dscr("resid", [2, 128, KC, 512])
    kscr_d = dscr("kscr", [128, 4, NRUN * 128], BF16)
    vscr_d = dscr("vscr", [NRUN * 128, 512], BF16)

    P = Prog(nc)
    es = contextlib.ExitStack()
    with es:
        RP = Region(nc, es, "RP", 65536)
        RQ = Region(nc, es, "RQ", 32768)
        RS = Region(nc, es, "RS", 49152)
        RW = Region(nc, es, "RW", 32768)
        RC = Region(nc, es, "RC", 30720)
        pst = [es.enter_context(nc.psum_tensor("ps%d" % i, [128, 512], F32)) for i in range(8)]
        psR = [Res(excl=True) for _ in range(8)]
        ps_free = list(range(8))

        def ps_alloc():
            assert ps_free, "out of PSUM banks"
            return ps_free.pop(0)

        def ps_release(i):
            ps_free.append(i)

        coff = [0]

        def cbuf(dt, shape):
            n = int(np.prod(shape)) * (4 if dt == F32 else 2)
            n_al = (n + 31) // 32 * 32
            b = Buf(RC, dt, coff[0], shape)
            coff[0] += n_al
            return b
        modT = cbuf(F32, [DEPTH * 2, 192])
        s1p = cbuf(F32, [DEPTH * 2, KC])
        s2a = cbuf(F32, [DEPTH * 2, KC])
        lnp = cbuf(F32, [DEPTH * 4, KC])
        lnA = cbuf(F32, [DEPTH * 2, KC])
        alng = cbuf(F32, [DEPTH, 8])
        bconv = cbuf(F32, [DEPTH * 3, 8])
        esink = cbuf(F32, [DEPTH, 16])
        bias2 = cbuf(F32, [8, 128])
        wsT = cbuf(BF16, [8, 128])
        ident = cbuf(F32, [128])
        identb = cbuf(BF16, [128])
        onesb = cbuf(BF16, [128])
        ones32 = cbuf(F32, [128])
        rmat = cbuf(F32, [128])
        maskb = cbuf(BF16, [2, 512])
        hfl = cbuf(F32, [8])
        ckT = cbuf(BF16, [4, 512])
        cvb = cbuf(BF16, [4, 512])
        hB = cbuf(BF16, [KC, 16])
        bcB = cbuf(F32, [8, 16])
        zB = cbuf(F32, [8, 16])
        svec = cbuf(BF16, [KC, 2])
        cvec = cbuf(F32, [KC, 2])
        bada = cbuf(F32, [192])
        st6 = cbuf(F32, [2, 6])
        mv = cbuf(F32, [4])
        assert coff[0] <= RC.nbytes, coff[0]

        def col(buf, i, j):
            return buf.ap[:, i, j:j + 1]

        wslots = [Buf(RW, BF16, i * 8192, [4096]) for i in range(4)]
        wrr = [0]

        def wload(src_ap, nk, ncols):
            s = wslots[wrr[0] % 4]
            wrr[0] += 1
            view = s.ap[:, 0:nk * ncols].rearrange("p (k n) -> p k n", k=nk)
            src = src_ap.rearrange("(k p) n -> p k n", p=128)
            P.op("pool", "dma_start", out=view, in_=src, writes=s.res, dma=True)
            return view, s.res

        deferred = []

        def flush_deferred():
            while deferred:
                deferred.pop(0)()

        def stream_fm(wsrc, nk, ncols, rhs_fn, N, evac, defer_evac_pe=None):
            nout = ncols // 128
            pk = min(nk, 4096 // ncols)
            banks = [ps_alloc() for _ in range(nout)]
            for k0 in range(0, nk, pk):
                wv, wres = wload(wsrc(k0, pk), pk, ncols)
                for j in range(nout):
                    b = banks[j]
                    for kl in range(pk):
                        kc = k0 + kl
                        rhs, rres = rhs_fn(kc)
                        P.op("pe", "matmul",
                            pst[b][:, 0:N], wv[:, kl, j * 128:(j + 1) * 128], rhs,
                            start=(kc == 0), stop=(kc == nk - 1),
                            reads=list(wres) + list(rres), writes=[psR[b]])
            flush_deferred()
            for j in range(nout):
                evac(j, banks[j])
                ps_release(banks[j])

        def stream_tm(wsrc, nk, ncols, lhs_fn, nblk, evac):
            pk = min(nk, 4096 // ncols)
            banks = [ps_alloc() for _ in range(nblk)]
            for k0 in range(0, nk, pk):
                wv, wres = wload(wsrc(k0, pk), pk, ncols)
                for t in range(nblk):
                    b = banks[t]
                    for kl in range(pk):
                        kc = k0 + kl
                        lhs, lres = lhs_fn(kc, t)
                        P.op("pe", "matmul",
                            pst[b][:, 0:ncols], lhs, wv[:, kl, :],
                            start=(kc == 0), stop=(kc == nk - 1),
                            reads=list(wres) + list(lres), writes=[psR[b]])
            flush_deferred()
            for t in range(nblk):
                evac(t, banks[t])
                ps_release(banks[t])

        def wsrc2(w3, l, c0, ncols, r0=0):
            return lambda k0, n: w3[l, r0 + k0 * 128: r0 + (k0 + n) * 128, c0:c0 + ncols]

        def sp_load(buf_ap, res, src):
            P.op("sp", "dma_start", out=buf_ap, in_=src, writes=res, dma=True)

        sp_load(ident.ap, ident.res, ident_d)
        sp_load(rmat.ap, rmat.res, rmat_d)
        sp_load(hfl.ap[:, 0:4], hfl.res, hfl_d)
        sp_load(cvec.ap, cvec.res, cvec_d)
        sp_load(lnp.ap.rearrange("p (l f) k -> p l f k", l=DEPTH), lnp.res, lnp_d)
        sp_load(alng.ap, alng.res, alng_d)
        sp_load(bconv.ap.rearrange("p (l f) k -> p l f k", l=DEPTH), bconv.res, bconv_d)
        sp_load(esink.ap, esink.res, sinkb_d)
        P.op("pool", "dma_start", out=maskb.ap, in_=maskb_d, writes=maskb.res, dma=True)
        P.op("dve", "memset", onesb.ap, 1.0, writes=onesb.res)
        P.op("dve", "memset", ones32.ap, 1.0, writes=ones32.res)
        P.op("dve", "memset", hfl.ap[:, 4:5], NEG, reads=hfl.res, writes=hfl.res)
        P.op("dve", "memset", hfl.ap[:, 5:6], EPS, reads=hfl.res, writes=hfl.res)
        P.op("dve", "memset", hfl.ap[:, 6:7], 0.0, reads=hfl.res, writes=hfl.res)
        P.op("dve", "tensor_copy", identb.ap, ident.ap, reads=ident.res, writes=identb.res)
        P.op("act", "activation", esink.ap, esink.ap, AF.Exp, reads=esink.res, writes=esink.res)
        P.op("act", "activation", svec.ap, cvec.ap, AF.Silu, reads=cvec.res, writes=svec.res)
        P.op("dve", "tensor_scalar", lnA.ap.rearrange("p (l f) k -> p l f k", l=DEPTH),
                                              lnp.ap.rearrange("p (l f) k -> p l f k", l=DEPTH)[:, :, 0:2, :],
                                              ALPHA, None, op0=ALU.mult, reads=lnp.res, writes=lnA.res)

        def mod_stage(l):
            sp_load(bada.ap, bada.res, bada_d[l])
            for cg in range(48):
                def evac(j, b, cg=cg):
                    c = cg * 4 + j
                    P.op("dve", "tensor_scalar",
                        modT.ap[:, 2 * l:2 * l + 2, c], pst[b][:, 0:2], bada.ap[:, c:c + 1], None, op0=ALU.add,
                        reads=[psR[b]] + bada.res, writes=modT.res)
                stream_fm(wsrc2(w_ada, l, cg * 512, 512), KC, 512,
                          lambda kc: (svec.ap[:, kc, :], svec.res), 2, evac)
            for g in range(2):
                i = 2 * l + g
                P.op("dve", "tensor_scalar", s1p.ap[:, i, :], modT.ap[:, i, 32:64], 1.0, None,
                                                         op0=ALU.add, reads=modT.res, writes=s1p.res)
                P.op("dve", "tensor_scalar", s2a.ap[:, i, :], modT.ap[:, i, 128:160], 1.0, 1.0 / ALPHA,
                                                         op0=ALU.add, op1=ALU.mult, reads=modT.res, writes=s2a.res)

        dbg_d = dout("dbg", [128, 4096]) if cfg.get("debug") else None

        def dbg_dump(ap, res, n):
            P.op("sp", "dma_start", out=dbg_d[:, 0:n], in_=ap, reads=res, dma=True)

        for l in layers:
            mod_stage(l)
        if cfg.get("stop") == "mod":
            dbg_dump(modT.ap.rearrange("p a b -> p (a b)"), modT.res, 768)
            P.emit()
            return nc

        def layer_setup(l):
            P.op("pool", "dma_start", out=wsT.ap, in_=awsT_d[l], writes=wsT.res, dma=True)
            ws32 = Buf(RS, F32, 0, [8, 128])
            L2 = Buf(RS, F32, 4096, [8, 128])
            R2 = Buf(RS, F32, 8192, [8, 128])
            sp_load(ws32.ap, ws32.res, awsT_d[l])
            P.op("dve", "memset", L2.ap[0:2], 1.0, writes=L2.res)
            P.op("sp", "dma_start", out=L2.ap[0:1], in_=alnb_d[l].rearrange("o (g c) -> o g c", g=8),
                 reads=L2.res, writes=L2.res, dma=True)
            P.op("sp", "dma_start", out=R2.ap[1:2], in_=abs_d[l].rearrange("o (g c) -> o g c", g=8),
                 writes=R2.res, dma=True)
            for g0 in range(0, 8, 4):
                b = ps_alloc()
                for g in range(g0, g0 + 4):
                    P.op("pe", "matmul", pst[b][0:1, (g - g0) * 128:(g - g0 + 1) * 128],
                                                         ones32.ap[:, 0:1], ws32.ap[:, g, :], start=True, stop=True,
                         reads=ones32.res + ws32.res, writes=[psR[b]])
                P.op("dve", "tensor_copy",
                    R2.ap[0:1, g0:g0 + 4, :], pst[b][0:1, :].rearrange("p (a c) -> p a c", a=4),
                    reads=[psR[b]] + R2.res, writes=R2.res)
                ps_release(b)
            for g0 in range(0, 8, 4):
                b = ps_alloc()
                for g in range(g0, g0 + 4):
                    P.op("pe", "matmul", pst[b][:, (g - g0) * 128:(g - g0 + 1) * 128],
                                                         L2.ap[0:2, g, :], R2.ap[0:2, g, :], start=True, stop=True,
                         reads=L2.res + R2.res, writes=[psR[b]])
                P.op("dve", "tensor_copy",
                    bias2.ap[:, g0:g0 + 4, :], pst[b][:, :].rearrange("p (a c) -> p a c", a=4),
                    reads=[psR[b]], writes=bias2.res)
                ps_release(b)

        hT = Buf(RP, BF16, 0, [KC, 512])
        yT = Buf(RP, BF16, 32768, [KC, 512])
        y1T = Buf(RP, F32, 0, [KC, 512])
        mT = Buf(RQ, BF16, 0, [KC, 512])

        def stage_x(src_rows, nb, mi, spill_par, dep_res=()):
            T = nb * 128
            xrow = [Buf(RS, F32, b * 16384, [D]) if b < 3 else Buf(RQ, F32, 0, [D]) for b in range(nb)]
            rst = [Buf(RQ, F32, 16384 + i * 2048, [512]) for i in range(4)]
            for b in range(nb):
                P.op("sp", "dma_start", out=xrow[b].ap, in_=src_rows(b),
                     reads=list(dep_res[b:b + 1]), writes=xrow[b].res, dma=True)
            for j in range(KC):
                bk = ps_alloc()
                for b in range(nb):
                    P.op("pe", "transpose",
                        pst[bk][:, b * 128:(b + 1) * 128], xrow[b].ap[:, j * 128:(j + 1) * 128], ident.ap,
                        reads=xrow[b].res[j // 2:j // 2 + 1] + ident.res, writes=[psR[bk]])
                P.op("act", "activation",
                    hT.ap[:, j, 0:T], pst[bk][:, 0:T], AF.Identity, bias=col(modT, mi, j), scale=col(s1p, mi, j),
                    reads=[psR[bk]] + modT.res + s1p.res, writes=hT.sub(j))
                if spill_par is not None:
                    r = rst[j % 4]
                    P.op("dve", "tensor_scalar", r.ap[:, 0:T], pst[bk][:, 0:T], ALPHA, None,
                                                                    op0=ALU.mult, reads=[psR[bk]], writes=r.res)
                    P.op("sp", "dma_start", out=resid_d[spill_par, :, j, 0:T], in_=r.ap[:, 0:T],
                         reads=r.res, writes=[residR[spill_par][j]], dma=True)
                ps_release(bk)

        residR = [[Res() for _ in range(KC)] for _ in range(2)]
        xmidR = {"p": [Res() for _ in range(8)], "s": [Res() for _ in range(NRUN)]}
        kscrR = [Res() for _ in range(NRUN)]
        vscrR = [Res() for _ in range(NRUN)]

        qTg = [Buf(RS, BF16, i * 4096, [4, 512]) for i in range(2)]
        kTt = Buf(RS, BF16, 8192, [4, 768])
        vt = Buf(RS, BF16, 14336, [6, 512])
        ptb = [Buf(RS, BF16, 20480 + i * 1024, [512]) for i in range(4)]
        f32t = [Buf(RS, F32, 24576 + i * 2048, [512]) for i in range(8)]
        ropet = Buf(RS, F32, 40960, [2, 512])

        def attention(g, kind, nb, l, b0, qbuf):
            for qb in range(nb):
                keys = []
                if kind == "p":
                    s = qb // 2
                    for kb in (2 * s, 2 * s + 1):
                        keys.append((kTt.ap[:, g, kb * 128:(kb + 1) * 128], kTt.res,
                                     vt.ap[:, kb, g * 128:(g + 1) * 128], vt.sub(kb), None, hfl.ap[:, 6:7]))
                else:
                    for w in range(3):
                        wi = qb + w
                        ab = b0 - 1 + wi
                        bias = hfl.ap[:, 6:7]
                        if ab == 0:
                            bias = hfl.ap[:, 0:1] if l == 0 else hfl.ap[:, 4:5]
                        elif ab == NRUN - 1:
                            bias = hfl.ap[:, 1:2] if l == 0 else hfl.ap[:, 4:5]
                        keys.append((kTt.ap[:, g, wi * 128:(wi + 1) * 128], kTt.res,
                                     vt.ap[:, wi, g * 128:(g + 1) * 128], vt.sub(wi),
                                     (0 if w == 0 else (1 if w == 2 else None)), bias))
                    for cb in range(4):
                        keys.append((ckT.ap[:, g, cb * 128:(cb + 1) * 128], ckT.res,
                                     cvb.ap[:, cb, g * 128:(g + 1) * 128], cvb.res, None, hfl.ap[:, 6:7]))
                nkb = len(keys)
                bN = ps_alloc()
                bD = ps_alloc()
                q_rhs = qbuf.ap[:, :, qb * 128:(qb + 1) * 128]
                sb = {}

                def do_s(i):
                    kT_ap, kres, _, _, mi_, _ = keys[i]
                    b = ps_alloc()
                    sb[i] = b
                    P.op("pe", "matmul", pst[b][:, :], kT_ap, q_rhs,
                                                  start=True, stop=(mi_ is None),
                         reads=list(kres) + qbuf.res, writes=[psR[b]])
                    if mi_ is not None:
                        P.op("pe", "matmul", pst[b][:, :], identb.ap, maskb.ap[:, mi_, :],
                                                      start=False, stop=True,
                             reads=identb.res + maskb.res, writes=[psR[b]])

                def do_pv(i):
                    _, _, v_ap, vres, _, bias = keys[i]
                    b = sb.pop(i)
                    pt = ptb[(qb * nkb + i) % 4]
                    P.op("act", "activation", pt.ap, pst[b][:, :], AF.Exp, bias=bias, scale=SCALE,
                         reads=[psR[b]] + hfl.res, writes=pt.res)
                    ps_release(b)
                    P.op("pe", "matmul", pst[bN][:, :], v_ap, pt.ap, start=(i == 0), stop=(i == nkb - 1),
                         reads=list(vres) + pt.res, writes=[psR[bN]])
                    P.op("pe", "matmul", pst[bD][:, :], onesb.ap, pt.ap, start=(i == 0), stop=(i == nkb - 1),
                         reads=onesb.res + pt.res, writes=[psR[bD]])
                do_s(0)
                if nkb > 1:
                    do_s(1)
                for i in range(nkb):
                    do_pv(i)
                    if i + 2 < nkb:
                        do_s(i + 2)
                den = f32t[6 + (qb % 2)]
                for h in range(4):
                    P.op("dve", "tensor_scalar",
                        den.ap[:, h * 128:(h + 1) * 128], pst[bD][:, h * 128:(h + 1) * 128],
                        col(esink, l, 4 * g + h), None, op0=ALU.add,
                        reads=[psR[bD]] + esink.res, writes=den.res)
                P.op("dve", "reciprocal", den.ap, den.ap, reads=den.res, writes=den.res)
                P.op("dve", "tensor_tensor",
                    yT.ap[:, 4 * g:4 * g + 4, qb * 128:(qb + 1) * 128],
                    pst[bN][:, :].rearrange("p (a c) -> p a c", a=4),
                    den.ap.rearrange("p (a c) -> p a c", a=4), op=ALU.mult,
                    reads=[psR[bN]] + den.res, writes=yT.subr(4 * g, 4 * g + 4))
                ps_release(bN)
                ps_release(bD)

        def rope_evac(dst_ap, dst_res, bank, T, tmp_i):
            qf = f32t[tmp_i % 2]
            t1 = f32t[2 + tmp_i % 2]
            t2 = f32t[4 + tmp_i % 2]
            P.op("act", "copy", qf.ap[:, 0:T], pst[bank][:, 0:T], reads=[psR[bank]], writes=qf.res)
            rb = ps_alloc()

            def pe_part():
                P.op("pe", "matmul", pst[rb][:, 0:T], rmat.ap, qf.ap[:, 0:T], start=True, stop=True,
                     reads=rmat.res + qf.res, writes=[psR[rb]])
                P.op("dve", "tensor_tensor", t2.ap[:, 0:T], pst[rb][:, 0:T], ropet.ap[:, 1, 0:T], op=ALU.mult,
                     reads=[psR[rb]] + ropet.res, writes=t2.res)
                ps_release(rb)
                P.op("dve", "tensor_tensor", t1.ap[:, 0:T], qf.ap[:, 0:T], ropet.ap[:, 0, 0:T], op=ALU.mult,
                     reads=qf.res + ropet.res, writes=t1.res)
                P.op("dve", "tensor_tensor", dst_ap, t1.ap[:, 0:T], t2.ap[:, 0:T], op=ALU.add,
                     reads=t1.res + t2.res, writes=dst_res)
            pe_part()

        MT = [(1, 4), (5, 4), (9, 4), (13, 4), (17, 1)]
        BND = []
        for (b0_, nb_) in MT:
            BND += [b0_ * 128 - 1, (b0_ + nb_) * 128]

        def prepass(l, src_d):
            mi = 2 * l + 1
            if l == 0:
                ptiles = [(0, 4), (4, 4), (8, 4), (12, 4), (16, 3)]
            else:
                ptiles = [(1, 4), (5, 4), (9, 4), (13, 4), (17, 1)]
            P.op("dve", "memset", hB.ap, 0.0, writes=hB.res)
            for (b0, nb) in ptiles:
                T = nb * 128
                stage_x(lambda b, b0=b0: src_d[(b0 + b) * 128:(b0 + b + 1) * 128, :], nb, mi, None,
                        [xmidR["s"][b0 + b] for b in range(nb)] if l > 0 else ())
                sp_load(ropet.ap[:, :, 0:T], ropet.res,
                        rope_d[:, :, b0 * 128:b0 * 128 + T].rearrange("a p t -> p a t"))
                for bi, tok in enumerate(BND):
                    if b0 * 128 <= tok < b0 * 128 + T:
                        P.op("dve", "tensor_copy",
                            hB.ap[:, :, bi:bi + 1], hT.ap[:, :, tok - b0 * 128:tok - b0 * 128 + 1],
                            reads=hT.res, writes=hB.res)
                kst = Buf(RS, BF16, 45056, [4, 512])

                def k_evac(j, bank, T=T, b0=b0, kst=kst):
                    rope_evac(kst.ap[:, j, 0:T], kst.res, bank, T, j)
                stream_fm(wsrc2(w_in, l, O_K, 512), KC, 512, lambda kc: (hT.ap[:, kc, 0:T], hT.sub(kc)), T, k_evac)
                P.op("sp", "dma_start",
                    out=kscr_d[:, :, b0 * 128:b0 * 128 + T], in_=kst.ap[:, :, 0:T],
                    reads=kst.res, writes=kscrR[b0:b0 + nb], dma=True)

                def v_evac(t, bank, b0=b0):
                    vs = ptb[t % 4]
                    P.op("act", "copy", vs.ap, pst[bank][:, :], reads=[psR[bank]], writes=vs.res)
                    P.op("sp", "dma_start", out=vscr_d[(b0 + t) * 128:(b0 + t + 1) * 128, :], in_=vs.ap,
                         reads=vs.res, writes=[vscrR[b0 + t]], dma=True)
                stream_tm(wsrc2(w_in, l, O_V, 512), KC, 512,
                          lambda kc, t: (hT.ap[:, kc, t * 128:(t + 1) * 128], hT.sub(kc)), nb, v_evac)
            for which, off in ((0, O_BC), (1, O_BH)):
                for grp in range(2):
                    def evac(j, bank, which=which, grp=grp):
                        c = grp * 4 + j
                        if which == 0:
                            P.op("act", "copy", bcB.ap[:, c, :], pst[bank][:, 0:16],
                                 reads=[psR[bank]], writes=bcB.res)
                        else:
                            P.op("dve", "tensor_tensor", zB.ap[:, c, :], pst[bank][:, 0:16], bcB.ap[:, c, :],
                                                                  op=ALU.mult,
                                 reads=[psR[bank]] + bcB.res, writes=zB.res)
                    stream_fm(wsrc2(w_in, l, off + grp * 512, 512), KC, 512,
                              lambda kc: (hB.ap[:, kc, :], hB.res), 16, evac)
            zv0 = hfl.ap[:, 2:3] if l == 0 else hfl.ap[:, 6:7]
            zv1 = hfl.ap[:, 3:4] if l == 0 else hfl.ap[:, 6:7]
            P.op("dve", "tensor_scalar", zB.ap[:, :, 0:1], zB.ap[:, :, 0:1], zv0, None, op0=ALU.mult,
                 reads=zB.res + hfl.res, writes=zB.res)
            P.op("dve", "tensor_scalar", zB.ap[:, :, 9:10], zB.ap[:, :, 9:10], zv1, None, op0=ALU.mult,
                 reads=zB.res + hfl.res, writes=zB.res)

        def sample_ctx_setup(l):
            P.op("pool", "dma_start", out=cvb.ap, in_=ckv_d[1, l].rearrange("(b p) n -> p b n", p=128),
                 writes=cvb.res, dma=True)
            ckf = Buf(RS, F32, 0, [4, 512])
            sp_load(ckf.ap, ckf.res, ckv_d[0, l].rearrange("(b p) n -> p b n", p=128))
            for cb in range(4):
                bk = ps_alloc()
                for h in range(4):
                    P.op("pe", "transpose",
                        pst[bk][:, h * 128:(h + 1) * 128], ckf.ap[:, cb, h * 128:(h + 1) * 128], ident.ap,
                        reads=ckf.res + ident.res, writes=[psR[bk]])
                P.op("act", "copy",
                    ckT.ap[:, :, cb * 128:(cb + 1) * 128], pst[bk][:, :].rearrange("p (a c) -> p a c", a=4),
                    reads=[psR[bk]], writes=ckT.res)
                ps_release(bk)

        tile_ctr = [0]

        def main_tile(l, kind, ti):
            par = tile_ctr[0] % 2
            tile_ctr[0] += 1
            if kind == "p":
                nb, b0 = 4, ti * 4
                mi = 2 * l
                src_d = xp_d if l == 0 else xmp_d
                dst_d = xmp_d if l < DEPTH - 1 else yp_d
                src_rows = lambda b: src_d[(b0 + b) * 128:(b0 + b + 1) * 128, :]
                dst_rows = lambda b: dst_d[(b0 + b) * 128:(b0 + b + 1) * 128, :]
                srcR = lambda b: xmidR["p"][b0 + b]
            else:
                b0, nb = MT[ti]
                mi = 2 * l + 1
                src_d = xs_d if l == 0 else xms_d
                src_rows = lambda b: src_d[(b0 + b) * 128:(b0 + b + 1) * 128, :]
                if l < DEPTH - 1:
                    dst_rows = lambda b: xms_d[(b0 + b) * 128:(b0 + b + 1) * 128, :]
                else:
                    dst_rows = lambda b: ys_d[(b0 - 1 + b) * 128:(b0 + b) * 128, :]
                srcR = lambda b: xmidR["s"][b0 + b]
            T = nb * 128
            gate1 = lambda c: col(modT, mi, 64 + c)
            shift2 = lambda c: col(modT, mi, 96 + c)
            gate2 = lambda c: col(modT, mi, 160 + c)
            hrhs = lambda kc: (hT.ap[:, kc, 0:T], hT.sub(kc))

            stage_x_deps = [srcR(b) for b in range(nb)] if l > 0 else []
            stage_x(src_rows, nb, mi, par, stage_x_deps)

            if kind == "p":
                def k_evac(t, bank):
                    kf = f32t[t % 2]
                    P.op("act", "copy", kf.ap, pst[bank][:, :], reads=[psR[bank]], writes=kf.res)
                    s, r = (b0 + t) // 2, ((b0 + t) % 2) * 128
                    P.op("sp", "dma_start", out=nk_d[s, l, r:r + 128, :], in_=kf.ap, reads=kf.res, dma=True)
                    bk = ps_alloc()
                    for h in range(4):
                        P.op("pe", "transpose", pst[bk][:, h * 128:(h + 1) * 128],
                                                             kf.ap[:, h * 128:(h + 1) * 128], ident.ap,
                             reads=kf.res + ident.res, writes=[psR[bk]])
                    P.op("dve", "tensor_copy", kTt.ap[:, :, t * 128:(t + 1) * 128],
                                                        pst[bk][:, :].rearrange("p (a c) -> p a c", a=4),
                         reads=[psR[bk]], writes=kTt.res)
                    ps_release(bk)
                stream_tm(wsrc2(w_in, l, O_K, 512), KC, 512,
                          lambda kc, t: (hT.ap[:, kc, t * 128:(t + 1) * 128], hT.sub(kc)), nb, k_evac)

                def v_evac(t, bank):
                    vf = f32t[2 + t % 2]
                    P.op("act", "copy", vf.ap, pst[bank][:, :], reads=[psR[bank]], writes=vf.res)
                    s, r = (b0 + t) // 2, ((b0 + t) % 2) * 128
                    P.op("sp", "dma_start", out=nv_d[s, l, r:r + 128, :], in_=vf.ap, reads=vf.res, dma=True)
                    P.op("dve", "tensor_copy", vt.ap[:, t, :], pst[bank][:, :],
                         reads=[psR[bank]], writes=vt.sub(t))
                stream_tm(wsrc2(w_in, l, O_V, 512), KC, 512,
                          lambda kc, t: (hT.ap[:, kc, t * 128:(t + 1) * 128], hT.sub(kc)), nb, v_evac)
            else:
                nw = nb + 2
                P.op("sp", "dma_start", out=kTt.ap[:, :, 0:nw * 128],
                                                 in_=kscr_d[:, :, (b0 - 1) * 128:(b0 - 1 + nw) * 128],
                     reads=kscrR[b0 - 1:b0 - 1 + nw], writes=kTt.res, dma=True)
                P.op("sp", "dma_start",
                    out=vt.ap[:, 0:nw, :],
                    in_=vscr_d[(b0 - 1) * 128:(b0 - 1 + nw) * 128, :].rearrange("(b p) n -> p b n", p=128),
                    reads=vscrR[b0 - 1:b0 - 1 + nw], writes=vt.res, dma=True)
                sp_load(ropet.ap[:, :, 0:T], ropet.res,
                        rope_d[:, :, b0 * 128:b0 * 128 + T].rearrange("a p t -> p a t"))

            for g in range(4):
                qbuf = qTg[g % 2]

                def q_evac(j, bank, qbuf=qbuf):
                    if kind == "p":
                        P.op("act", "copy", qbuf.ap[:, j, 0:T], pst[bank][:, 0:T],
                             reads=[psR[bank]], writes=qbuf.res)
                    else:
                        rope_evac(qbuf.ap[:, j, 0:T], qbuf.res, bank, T, j)
                stream_fm(wsrc2(w_in, l, O_Q + g * 512, 512), KC, 512, hrhs, T, q_evac)
                deferred.append(lambda g=g, qbuf=qbuf: attention(g, kind, nb, l, b0, qbuf))

            avs = Buf(RS, F32, 0, [4, 512])
            zA = Buf(RS, BF16, 8192, [4, 1024])
            tA = Buf(RS, F32, 24576, [8, 512])

            def av0_evac(t, bank):
                P.op("act", "copy", avs.ap[:, t, :], pst[bank][:, :], reads=[psR[bank]], writes=avs.sub(t))
            stream_tm(wsrc2(w_in, l, O_AV, 512), KC, 512,
                      lambda kc, t: (hT.ap[:, kc, t * 128:(t + 1) * 128], hT.sub(kc)), nb, av0_evac)

            def av1_evac(t, bank):
                P.op("dve", "bn_stats", st6.ap[:, 0, :], avs.ap[:, t, :], reads=avs.sub(t), writes=st6.res)
                P.op("dve", "bn_stats", st6.ap[:, 1, :], pst[bank][:, :], reads=[psR[bank]] + st6.res,
                     writes=st6.res)
                P.op("dve", "bn_aggr", mv.ap[:, 0:2], st6.ap, reads=st6.res, writes=mv.res)
                P.op("act", "activation", mv.ap[:, 2:3], mv.ap[:, 1:2], AF.Sqrt, bias=hfl.ap[:, 5:6], scale=1.0,
                     reads=mv.res + hfl.res, writes=mv.res)
                P.op("dve", "reciprocal", mv.ap[:, 3:4], mv.ap[:, 2:3], reads=mv.res, writes=mv.res)
                P.op("dve", "tensor_scalar", zA.ap[:, t, 0:512], avs.ap[:, t, :], mv.ap[:, 0:1], mv.ap[:, 3:4],
                                                      op0=ALU.subtract, op1=ALU.mult,
                     reads=avs.sub(t) + mv.res, writes=zA.sub(t))
                P.op("dve", "tensor_scalar", zA.ap[:, t, 512:1024], pst[bank][:, :], mv.ap[:, 0:1], mv.ap[:, 3:4],
                                                      op0=ALU.subtract, op1=ALU.mult,
                     reads=[psR[bank]] + mv.res, writes=zA.sub(t))
            stream_tm(wsrc2(w_in, l, O_AV + 512, 512), KC, 512,
                      lambda kc, t: (hT.ap[:, kc, t * 128:(t + 1) * 128], hT.sub(kc)), nb, av1_evac)

            def s_part():
                for g in range(8):
                    bk = ps_alloc()
                    for t in range(nb):
                        P.op("pe", "matmul",
                            pst[bk][:, t * 128:(t + 1) * 128], zA.ap[:, t, g * 128:(g + 1) * 128], wsT.ap[:, g, :],
                            start=True, stop=True, reads=zA.sub(t) + wsT.res, writes=[psR[bk]])
                    P.op("dve", "scalar_tensor_tensor",
                        tA.ap[:, g, 0:T].rearrange("p (a c) -> p a c", a=nb),
                        pst[bk][:, 0:T].rearrange("p (a c) -> p a c", a=nb), col(alng, l, g),
                        bias2.ap[:, g:g + 1, :].broadcast_to([128, nb, 128]), op0=ALU.mult, op1=ALU.add,
                        reads=[psR[bk]] + alng.res + bias2.res, writes=tA.sub(g))
                    ps_release(bk)
            deferred.append(s_part)

            for grp in range(2):
                def au_evac(j, bank, grp=grp):
                    c = grp * 4 + j
                    P.op("dve", "tensor_tensor", yT.ap[:, 16 + c, 0:T], pst[bank][:, 0:T], tA.ap[:, c, 0:T],
                                                          op=ALU.mult,
                         reads=[psR[bank]] + tA.sub(c), writes=yT.sub(16 + c))
                stream_fm(wsrc2(w_in, l, O_AU + grp * 512, 512), KC, 512, hrhs, T, au_evac)

            bhT = Buf(RS, F32, 0, [8, 512])
            zT = Buf(RS, F32, 16384, [8, 516])
            for grp in range(2):
                def bh_evac(j, bank, grp=grp):
                    c = grp * 4 + j
                    P.op("act", "copy", bhT.ap[:, c, 0:T], pst[bank][:, 0:T], reads=[psR[bank]],
                         writes=bhT.sub(c))
                stream_fm(wsrc2(w_in, l, O_BH + grp * 512, 512), KC, 512, hrhs, T, bh_evac)
            if kind == "s":
                P.op("dve", "tensor_copy", zT.ap[:, :, 0:1], zB.ap[:, :, 2 * ti:2 * ti + 1],
                     reads=zB.res, writes=zT.res)
                P.op("dve", "tensor_copy", zT.ap[:, :, T + 1:T + 2], zB.ap[:, :, 2 * ti + 1:2 * ti + 2],
                     reads=zB.res, writes=zT.res)
                segs = [(0, T, True)]
            else:
                segs = [(0, 256, False), (256, 512, False)]
            for grp in range(2):
                def bc_evac(j, bank, grp=grp):
                    c = grp * 4 + j
                    P.op("dve", "tensor_tensor", zT.ap[:, c, 1:T + 1], pst[bank][:, 0:T], bhT.ap[:, c, 0:T],
                                                          op=ALU.mult,
                         reads=[psR[bank]] + bhT.sub(c), writes=zT.sub(c))
                    w0, w1, w2 = col(bconv, 3 * l, c), col(bconv, 3 * l + 1, c), col(bconv, 3 * l + 2, c)
                    for (a, b_, halo) in segs:
                        P.op("dve", "tensor_scalar",
                            bhT.ap[:, c, a:b_], zT.ap[:, c, 1 + a:1 + b_], w1, None, op0=ALU.mult,
                            reads=zT.sub(c) + bconv.res, writes=bhT.sub(c))
                        lo = 0 if halo else 1
                        P.op("dve", "scalar_tensor_tensor",
                            bhT.ap[:, c, a + lo:b_], zT.ap[:, c, a + lo:b_], w0, bhT.ap[:, c, a + lo:b_],
                            op0=ALU.mult, op1=ALU.add,
                            reads=zT.sub(c) + bhT.sub(c) + bconv.res, writes=bhT.sub(c))
                        P.op("dve", "scalar_tensor_tensor",
                            bhT.ap[:, c, a:b_ - lo], zT.ap[:, c, a + 2:b_ + 2 - lo], w2, bhT.ap[:, c, a:b_ - lo],
                            op0=ALU.mult, op1=ALU.add,
                            reads=zT.sub(c) + bhT.sub(c) + bconv.res, writes=bhT.sub(c))
                stream_fm(wsrc2(w_in, l, O_BC + grp * 512, 512), KC, 512, hrhs, T, bc_evac)
            for grp in range(2):
                def bb_evac(j, bank, grp=grp):
                    c = grp * 4 + j
                    P.op("dve", "tensor_tensor", yT.ap[:, 24 + c, 0:T], pst[bank][:, 0:T], bhT.ap[:, c, 0:T],
                                                          op=ALU.mult,
                         reads=[psR[bank]] + bhT.sub(c), writes=yT.sub(24 + c))
                stream_fm(wsrc2(w_in, l, O_BB + grp * 512, 512), KC, 512, hrhs, T, bb_evac)

            sg = Buf(RS, F32, 0, [12, 512])
            mt = Buf(RS, F32, 24576, [4, 512])
            tmpm = Buf(RS, F32, 32768, [4, 512])
            for G in range(8):
                for gi, off in enumerate((O_GATT, O_GA, O_GB)):
                    def g_evac(j, bank, gi=gi):
                        P.op("act", "activation", sg.ap[:, gi * 4 + j, 0:T], pst[bank][:, 0:T], AF.Sigmoid,
                             reads=[psR[bank]], writes=sg.sub(gi * 4 + j))
                    stream_fm(wsrc2(w_in, l, off + G * 512, 512), KC, 512, hrhs, T, g_evac)

                def pa_evac(j, bank):
                    P.op("dve", "tensor_tensor", mt.ap[:, j, 0:T], pst[bank][:, 0:T], sg.ap[:, j, 0:T], op=ALU.mult,
                         reads=[psR[bank]] + sg.sub(j), writes=mt.sub(j))
                stream_fm(wsrc2(p_att, l, G * 512, 512), 16, 512, lambda kc: (yT.ap[:, kc, 0:T], yT.sub(kc)), T, pa_evac)

                def pb_evac(j, bank):
                    P.op("dve", "tensor_tensor", tmpm.ap[:, j, 0:T], pst[bank][:, 0:T], sg.ap[:, 4 + j, 0:T],
                                                          op=ALU.mult,
                         reads=[psR[bank]] + sg.sub(4 + j), writes=tmpm.sub(j))
                    P.op("dve", "tensor_tensor", mt.ap[:, j, 0:T], mt.ap[:, j, 0:T], tmpm.ap[:, j, 0:T], op=ALU.add,
                         reads=mt.sub(j) + tmpm.sub(j), writes=mt.sub(j))
                stream_fm(wsrc2(p_a, l, G * 512, 512), 8, 512, lambda kc: (yT.ap[:, 16 + kc, 0:T], yT.sub(16 + kc)), T,
                          pb_evac)

                def pc_evac(j, bank, G=G):
                    P.op("dve", "tensor_tensor", tmpm.ap[:, j, 0:T], pst[bank][:, 0:T], sg.ap[:, 8 + j, 0:T],
                                                          op=ALU.mult,
                         reads=[psR[bank]] + sg.sub(8 + j), writes=tmpm.sub(j))
                    P.op("dve", "tensor_tensor", mT.ap[:, 4 * G + j, 0:T], mt.ap[:, j, 0:T], tmpm.ap[:, j, 0:T],
                                                          op=ALU.add,
                         reads=mt.sub(j) + tmpm.sub(j), writes=mT.sub(4 * G + j))
                stream_fm(wsrc2(p_b, l, G * 512, 512), 8, 512, lambda kc: (yT.ap[:, 24 + kc, 0:T], yT.sub(24 + kc)), T,
                          pc_evac)

            rin = [Buf(RS, F32, i * 2048, [512]) for i in range(4)]
            sqt = [Buf(RS, F32, 8192 + i * 2048, [512]) for i in range(4)]
            stt = [Buf(RS, F32, 16384 + i * 2048, [512]) for i in range(4)]
            nrm = [Buf(RS, F32, 24576 + i * 2048, [512]) for i in range(4)]

            def ln_stats_mm(c, src_ap, src_res, bS, bQ, sq):
                P.op("act", "activation", sq.ap[:, 0:T], src_ap, AF.Square, reads=src_res, writes=sq.res)

                def pe_part():
                    P.op("pe", "matmul", pst[bS][:, 0:T], ones32.ap, src_ap, start=(c == 0), stop=(c == KC - 1),
                         reads=ones32.res + list(src_res), writes=[psR[bS]])
                    P.op("pe", "matmul", pst[bQ][:, 0:T], ones32.ap, sq.ap[:, 0:T], start=(c == 0),
                                                  stop=(c == KC - 1),
                         reads=ones32.res + sq.res, writes=[psR[bQ]])
                deferred.append(pe_part)

            def ln_finish(bS, bQ):
                mean, rstd, msq = stt[0], stt[1], stt[2]
                P.op("dve", "tensor_scalar", mean.ap[:, 0:T], pst[bS][:, 0:T], 1.0 / D, None, op0=ALU.mult,
                     reads=[psR[bS]], writes=mean.res)
                P.op("dve", "tensor_tensor", msq.ap[:, 0:T], mean.ap[:, 0:T], mean.ap[:, 0:T], op=ALU.mult,
                     reads=mean.res, writes=msq.res)
                P.op("dve", "scalar_tensor_tensor", rstd.ap[:, 0:T], pst[bQ][:, 0:T], 1.0 / D, msq.ap[:, 0:T],
                                                             op0=ALU.mult, op1=ALU.subtract,
                     reads=[psR[bQ]] + msq.res, writes=rstd.res)
                P.op("act", "activation", rstd.ap[:, 0:T], rstd.ap[:, 0:T], AF.Sqrt, bias=hfl.ap[:, 5:6], scale=1.0,
                     reads=rstd.res + hfl.res, writes=rstd.res)
                P.op("dve", "reciprocal", rstd.ap[:, 0:T], rstd.ap[:, 0:T], reads=rstd.res, writes=rstd.res)
                ps_release(bS)
                ps_release(bQ)
                return mean, rstd

            bS, bQ = ps_alloc(), ps_alloc()
            for G in range(8):
                for j in range(4):
                    c = 4 * G + j
                    P.op("sp", "dma_start", out=rin[j].ap[:, 0:T], in_=resid_d[par, :, c, 0:T],
                         reads=[residR[par][c]], writes=rin[j].res, dma=True)

                def o_evac(j, bank, G=G):
                    c = 4 * G + j
                    P.op("dve", "scalar_tensor_tensor", y1T.ap[:, c, 0:T], pst[bank][:, 0:T], gate1(c),
                                                                 rin[j].ap[:, 0:T], op0=ALU.mult, op1=ALU.add,
                         reads=[psR[bank]] + rin[j].res + modT.res, writes=y1T.sub(c))
                    ln_stats_mm(c, y1T.ap[:, c, 0:T], y1T.sub(c), bS, bQ, sqt[c % 4])
                stream_fm(wsrc2(w_o, l, G * 512, 512), KC, 512, lambda kc: (mT.ap[:, kc, 0:T], mT.sub(kc)), T, o_evac)
            flush_deferred()
            mean, rstd = ln_finish(bS, bQ)
            for c in range(KC):
                t1, t2 = nrm[c % 2], nrm[2 + c % 2]
                P.op("dve", "tensor_tensor", t1.ap[:, 0:T], y1T.ap[:, c, 0:T], mean.ap[:, 0:T],
                                                                  op=ALU.subtract,
                     reads=y1T.sub(c) + mean.res, writes=t1.res)
                P.op("dve", "scalar_tensor_tensor",
                    t2.ap[:, 0:T], t1.ap[:, 0:T], col(lnA, 2 * l, c), rstd.ap[:, 0:T], op0=ALU.mult, op1=ALU.mult,
                    reads=t1.res + rstd.res + lnA.res, writes=t2.res)
                P.op("act", "activation", y1T.ap[:, c, 0:T], t2.ap[:, 0:T], AF.Identity,
                                                             bias=col(lnA, 2 * l + 1, c), scale=1.0,
                     reads=t2.res + lnA.res, writes=y1T.sub(c))
                P.op("act", "activation", mT.ap[:, c, 0:T], y1T.ap[:, c, 0:T], AF.Identity,
                                                      bias=shift2(c), scale=col(s2a, mi, c),
                     reads=y1T.sub(c) + modT.res + s2a.res, writes=mT.sub(c))

            hid = [Buf(RS, BF16, i * 16384, [16, 512]) for i in range(2)]
            rtmp = [Buf(RS, F32, 32768 + i * 2048, [512]) for i in range(2)]
            h2rhs = lambda kc: (mT.ap[:, kc, 0:T], mT.sub(kc))
            for fb in range(8):
                hb = hid[fb % 2]
                for ug in range(4):
                    def up_evac(j, bank, ug=ug, hb=hb):
                        c = 4 * ug + j
                        rt = rtmp[c % 2]
                        P.op("act", "activation", rt.ap[:, 0:T], pst[bank][:, 0:T], AF.Relu,
                             reads=[psR[bank]], writes=rt.res)
                        P.op("dve", "tensor_tensor", hb.ap[:, c, 0:T], rt.ap[:, 0:T], rt.ap[:, 0:T], op=ALU.mult,
                             reads=rt.res, writes=hb.sub(c))
                    stream_fm(wsrc2(w_up, l, fb * 2048 + ug * 512, 512), KC, 512, h2rhs, T, up_evac)
                if fb == 7:
                    sq2 = [Buf(RS, F32, 36864 + i * 2048, [512]) for i in range(4)]
                    bS2, bQ2 = ps_alloc(), ps_alloc()
                for dg in range(8):
                    def dn_evac(j, bank, dg=dg, fb=fb):
                        c = 4 * dg + j
                        P.op("dve", "scalar_tensor_tensor", y1T.ap[:, c, 0:T], pst[bank][:, 0:T], gate2(c),
                                                                     y1T.ap[:, c, 0:T], op0=ALU.mult, op1=ALU.add,
                             reads=[psR[bank]] + y1T.sub(c) + modT.res, writes=y1T.sub(c))
                        if fb == 7:
                            ln_stats_mm(c, y1T.ap[:, c, 0:T], y1T.sub(c), bS2, bQ2, sq2[c % 4])
                    stream_fm(wsrc2(w_down, l, dg * 512, 512, r0=fb * 2048), 16, 512,
                              lambda kc, hb=hb: (hb.ap[:, kc, 0:T], hb.sub(kc)), T, dn_evac)

            flush_deferred()
            stt2 = [Buf(RS, F32, 45056, [512]), Buf(RS, F32, 47104, [512]), Buf(RS, F32, 32768, [512])]
            stt[0], stt[1], stt[2] = stt2
            mean, rstd = ln_finish(bS2, bQ2)
            nrm2 = [Buf(RS, F32, 34816 + i * 2048, [512]) for i in range(3)]
            for c in range(KC):
                t1 = nrm2[c % 3]
                P.op("dve", "tensor_tensor", t1.ap[:, 0:T], y1T.ap[:, c, 0:T], mean.ap[:, 0:T],
                                                                  op=ALU.subtract,
                     reads=y1T.sub(c) + mean.res, writes=t1.res)
                P.op("dve", "scalar_tensor_tensor",
                    t1.ap[:, 0:T], t1.ap[:, 0:T], col(lnp, 4 * l + 2, c), rstd.ap[:, 0:T], op0=ALU.mult, op1=ALU.mult,
                    reads=t1.res + rstd.res + lnp.res, writes=t1.res)
                P.op("act", "activation", y1T.ap[:, c, 0:T], t1.ap[:, 0:T], AF.Identity,
                                                             bias=col(lnp, 4 * l + 3, c), scale=1.0,
                     reads=t1.res + lnp.res, writes=y1T.sub(c))
            ost = [Buf(RS, F32, i * 16384, [D]) for i in range(2)]
            for b in range(nb):
                o = ost[b % 2]
                for G in range(8):
                    bk = ps_alloc()
                    for j in range(4):
                        c = 4 * G + j
                        P.op("pe", "transpose",
                            pst[bk][:, j * 128:(j + 1) * 128], y1T.ap[:, c, b * 128:(b + 1) * 128], ident.ap,
                            reads=y1T.sub(c) + ident.res, writes=[psR[bk]])
                    if G % 2 == 0:
                        P.op("act", "copy", o.ap[:, G * 512:(G + 1) * 512], pst[bk][:, :],
                             reads=[psR[bk]], writes=o.res[2 * G:2 * G + 2])
                    else:
                        P.op("dve", "tensor_copy", o.ap[:, G * 512:(G + 1) * 512], pst[bk][:, :],
                             reads=[psR[bk]], writes=o.res[2 * G:2 * G + 2])
                    ps_release(bk)
                P.op("sp", "dma_start", out=dst_rows(b), in_=o.ap, reads=o.res,
                     writes=[srcR(b)], dma=True)

        if cfg.get("group_major", True):
            if do_prompt:
                for l in layers:
                    layer_setup(l)
                    for ti in do_prompt:
                        main_tile(l, "p", ti)
            if do_sample:
                for l in layers:
                    layer_setup(l)
                    prepass(l, xs_d if l == 0 else xms_d)
                    sample_ctx_setup(l)
                    for ti in cfg.get("sample_tiles", range(5)):
                        main_tile(l, "s", ti)
        else:
            for l in layers:
                layer_setup(l)
                if do_sample:
                    prepass(l, xs_d if l == 0 else xms_d)
                for ti in do_prompt:
                    main_tile(l, "p", ti)
                if do_sample:
                    sample_ctx_setup(l)
                    for ti in cfg.get("sample_tiles", range(5)):
                        main_tile(l, "s", ti)
        flush_deferred()
        P.emit()
    return nc


def _rope_tables(base_blk):
    pos = (base_blk * 128 + np.arange(NRUN * 128)).astype(np.int64)
    pos = np.clip(pos, 0, 4095)
    row = (pos // 64).astype(np.float32)
    colp = (pos % 64).astype(np.float32)
    inv = (1.0 / (np.float32(10000.0) ** (np.arange(0, 64, 2, dtype=np.float32) / np.float32(64)))).astype(np.float32)
    ang_r = (row[:, None] * inv[None, :]).astype(np.float32)
    ang_c = (colp[:, None] * inv[None, :]).astype(np.float32)
    ang = np.concatenate([ang_r, ang_r, ang_c, ang_c], axis=1)
    tab = np.stack([np.cos(ang).astype(np.float32).T, np.sin(ang).astype(np.float32).T], 0)
    return np.ascontiguousarray(tab)


def _rmat():
    m = np.zeros((128, 128), np.float32)
    for d in range(128):
        if d % 64 < 32:
            m[d + 32, d] = -1.0
        else:
            m[d - 32, d] = 1.0
    return m


def _masks():
    k = np.arange(128)[:, None]
    q = np.arange(128)[None, :]
    mp = np.where(k >= q, 0.0, NEG).astype(np.float32)
    mn = np.where(k <= q, 0.0, NEG).astype(np.float32)
    out = np.zeros((128, 2, 512), np.float32)
    out[:, 0, :] = np.tile(mp, (1, 4))
    out[:, 1, :] = np.tile(mn, (1, 4))
    return out


def prep_inputs(inp, n_cores=8):
    f = lambda a: np.ascontiguousarray(np.asarray(a, dtype=np.float32))
    x_prompt, x_sample = f(inp["x_prompt"]), f(inp["x_sample"])
    cache_k, cache_v = f(inp["cache_k"]), f(inp["cache_v"])
    c, c_ctx = f(inp["c"]), f(inp["c_ctx"])
    shared = {k: f(inp[k]) for k in ("w_ada", "w_in", "p_att", "p_a", "p_b", "w_o", "w_up", "w_down")}
    fm = lambda v, n: np.ascontiguousarray(v.reshape(n, 128).T)
    b_ada = f(inp["b_ada"])
    shared["bada"] = np.stack([fm(b_ada[l], 192) for l in range(DEPTH)], 0)
    lnp = np.zeros((128, DEPTH, 4, KC), np.float32)
    for l in range(DEPTH):
        for i, k in enumerate(("ln1_g", "ln1_b", "ln2_g", "ln2_b")):
            lnp[:, l, i, :] = fm(f(inp[k])[l], KC)
    shared["lnp"] = lnp
    shared["alng"] = np.ascontiguousarray(np.stack([fm(f(inp["a_ln_g"])[l], 8) for l in range(DEPTH)], 1))
    shared["alnb"] = f(inp["a_ln_b"]).reshape(DEPTH, 1, 1024)
    a_ws = f(inp["a_ws"])
    shared["awsT"] = np.ascontiguousarray(a_ws.transpose(0, 3, 1, 2))
    shared["abs"] = f(inp["a_bs"]).reshape(DEPTH, 1, 1024)
    b_conv = f(inp["b_conv"])
    bc = np.zeros((128, DEPTH, 3, 8), np.float32)
    for l in range(DEPTH):
        for i in range(3):
            bc[:, l, i, :] = fm(b_conv[l, i], 8)
    shared["bconv"] = bc
    shared["sinkb"] = np.ascontiguousarray(np.broadcast_to(f(inp["w_sink"])[None], (128, DEPTH, 16)))
    shared["rmat"] = _rmat()
    shared["ident"] = np.eye(128, dtype=np.float32)
    shared["maskb"] = _masks()
    ropes = {-1: _rope_tables(-1), 14: _rope_tables(14)}
    maps = []
    for core in range(n_cores):
        b, half = core // 2, core % 2
        base = -1 if half == 0 else 14
        xs = np.zeros((NRUN * 128, D), np.float32)
        for i in range(NRUN):
            sb = base + i
            if 0 <= sb < 32:
                xs[i * 128:(i + 1) * 128] = x_sample[b, sb * 128:(sb + 1) * 128]
        hfl = np.zeros((128, 4), np.float32)
        if half == 0:
            hfl[:, 0], hfl[:, 1], hfl[:, 2], hfl[:, 3] = NEG, 0.0, 0.0, 1.0
        else:
            hfl[:, 0], hfl[:, 1], hfl[:, 2], hfl[:, 3] = 0.0, NEG, 1.0, 0.0
        cvec = np.zeros((128, KC, 2), np.float32)
        cvec[:, :, 0] = fm(c_ctx, KC)
        cvec[:, :, 1] = fm(c[b], KC)
        m = dict(shared)
        m["xp"] = np.ascontiguousarray(x_prompt[4 * core:4 * core + 4].reshape(1024, D))
        m["xs"] = xs
        m["ckv"] = np.ascontiguousarray(np.stack([cache_k[b].reshape(DEPTH, 512, 512),
                                                  cache_v[b].reshape(DEPTH, 512, 512)], 0))
        m["cvec"] = cvec
        m["rope"] = ropes[base]
        m["hfl"] = hfl
        maps.append(m)
    return maps


def assemble(results, n_cores=8):
    y_prompt = np.zeros((32, 256, D), np.float32)
    y_sample = np.zeros((4, 4096, D), np.float32)
    new_k = np.zeros((32, DEPTH, 256, 4, 128), np.float32)
    new_v = np.zeros((32, DEPTH, 256, 4, 128), np.float32)
    for core in range(n_cores):
        r = results[core]
        b, half = core // 2, core % 2
        y_prompt[4 * core:4 * core + 4] = r["yp"].reshape(4, 256, D)
        if half == 0:
            y_sample[b, 0:2048] = r["ys"][0:2048]
        else:
            y_sample[b, 2048:4096] = r["ys"][128:2176]
        new_k[4 * core:4 * core + 4] = r["nk"].reshape(4, DEPTH, 256, 4, 128)
        new_v[4 * core:4 * core + 4] = r["nv"].reshape(4, DEPTH, 256, 4, 128)
    return y_prompt, y_sample, new_k, new_v


def kernel(**inputs):
    maps = prep_inputs(inputs)
    nc = build()
    res = run_bass_kernel_spmd(nc, maps, core_ids=list(range(8)))
    return assemble(res.results)
```
